# Optimizing a Trainium2 kernel written in Bass

```python
import jax, jax.numpy as jnp
from jax import lax
import numpy as np

D_MODEL = 2048
BATCH = 4
SEQ = 4096
DEPTH = 2

MEM_LEN = 256
HEAD_DIM = 128
MIX_W = D_MODEL
MEM_HEADS = 4
MEM_W = MEM_HEADS * HEAD_DIM
POOL_W = MIX_W - MEM_W
POOL_WINDOWS = (2, 4, 8, 16)
POOL_GROUPS = len(POOL_WINDOWS)
POOL_GC = POOL_W // POOL_GROUPS
NSA_W = MIX_W - MEM_W
NSA_HEADS = NSA_W // HEAD_DIM
NSA_KV_HEADS = 4
NSA_GROUP = NSA_HEADS // NSA_KV_HEADS
NSA_KV_W = NSA_KV_HEADS * HEAD_DIM
CMP_LEN = 32
CMP_STRIDE = 16
CMP_HID = 256
SEL_LEN = 64
SEL_TOPK = 16
WINDOW = 512
Q_CHUNK = 32
ROT_DIM = HEAD_DIM // 4
ROPE_THETA = 500000.0
NORM_EPS = 1e-6
FORCE_SCORE = 1e4
NEG_INF = -1e30

kernel_name = "yoco_pool_nsa_memory_hybrid"


def rmsnorm(x, g):
    xf = x.astype(jnp.float32)
    y = xf * lax.rsqrt(jnp.mean(xf * xf, axis=-1, keepdims=True) + NORM_EPS)
    return (y * g.astype(jnp.float32)).astype(x.dtype)


def rope(x, pos):
    half = ROT_DIM // 2
    inv = ROPE_THETA ** (-jnp.arange(half, dtype=jnp.float32) * 2.0 / ROT_DIM)
    ang = pos.astype(jnp.float32)[..., None] * inv
    cos = jnp.cos(ang)[:, :, None, :]
    sin = jnp.sin(ang)[:, :, None, :]
    xr = x[..., :ROT_DIM].astype(jnp.float32)
    x1, x2 = xr[..., :half], xr[..., half:]
    rot = jnp.concatenate([x1 * cos - x2 * sin, x2 * cos + x1 * sin], axis=-1)
    return jnp.concatenate([rot.astype(x.dtype), x[..., ROT_DIM:]], axis=-1)


def masked_softmax(s, mask):
    s = jnp.where(mask, s.astype(jnp.float32), NEG_INF)
    p = jax.nn.softmax(s, axis=-1)
    return jnp.where(mask, p, 0.0)


def memory_attention(q, mem, mem_g, w_mem_kv):
    b, m, _ = mem.shape
    mn = rmsnorm(mem, mem_g)
    kv = (mn @ w_mem_kv).reshape(b, m, 2, MEM_HEADS, HEAD_DIM)
    k, v = kv[:, :, 0], kv[:, :, 1]
    s = jnp.einsum('bshd,bmhd->bhsm', q, k).astype(jnp.float32)
    p = jax.nn.softmax(s, axis=-1).astype(v.dtype)
    o = jnp.einsum('bhsm,bmhd->bshd', p, v)
    return o.reshape(q.shape[0], q.shape[1], MEM_W)


def multiscale_pool(u, w_pool, scale):
    b, s, _ = u.shape
    ug = u.reshape(b, s, POOL_GROUPS, POOL_GC)
    t1 = jnp.arange(1, s + 1, dtype=jnp.float32)[None, :, None]
    outs = []
    for gi, win in enumerate(POOL_WINDOWS):
        v = ug[:, :, gi].astype(jnp.float32)
        c = jnp.cumsum(v, axis=1)
        c_lag = jnp.pad(c, ((0, 0), (win, 0), (0, 0)))[:, :s]
        outs.append((c - c_lag) / jnp.minimum(t1, float(win)) - v)
    pooled = jnp.stack(outs, axis=2).astype(u.dtype)
    mixed = jnp.einsum('bsgc,gcd->bsgd', pooled, w_pool).reshape(b, s, POOL_W)
    return mixed * scale


def nsa_shared_kv(h, kv_norm_g, w_kv, cmp_pe, cmp_w1, cmp_w2, positions):
    b, s, _ = h.shape
    hn = rmsnorm(h, kv_norm_g)
    kv = (hn @ w_kv).reshape(b, s, 6, NSA_KV_HEADS, HEAD_DIM)
    k_c, v_c, k_s, v_s, k_w, v_w = [kv[:, :, i] for i in range(6)]
    k_s = rope(k_s, positions)
    k_w = rope(k_w, positions)
    n_cmp = (s - CMP_LEN) // CMP_STRIDE + 1
    blk = jnp.arange(n_cmp)[:, None] * CMP_STRIDE + jnp.arange(CMP_LEN)[None, :]

    def compress(z, pe, w1, w2):
        zb = z[:, blk] + pe[None, None, :, None, :]
        zb = zb.transpose(0, 1, 3, 2, 4).reshape(b, n_cmp, NSA_KV_HEADS, CMP_LEN * HEAD_DIM)
        return jax.nn.silu(zb @ w1) @ w2

    k_cmp = compress(k_c, cmp_pe[0], cmp_w1[0], cmp_w2[0])
    v_cmp = compress(v_c, cmp_pe[1], cmp_w1[1], cmp_w2[1])
    k_cmp = rope(k_cmp, positions[:, blk[:, -1]])
    n_sel = s // SEL_LEN
    k_blk = k_s.reshape(b, n_sel, SEL_LEN, NSA_KV_HEADS, HEAD_DIM).transpose(0, 3, 1, 2, 4)
    v_blk = v_s.reshape(b, n_sel, SEL_LEN, NSA_KV_HEADS, HEAD_DIM).transpose(0, 3, 1, 2, 4)
    pad = ((0, 0), (WINDOW, 0), (0, 0), (0, 0))
    return (k_cmp, v_cmp, k_blk, v_blk, jnp.pad(k_w, pad), jnp.pad(v_w, pad))


def nsa_attention(q, gates, k_cmp, v_cmp, k_blk, v_blk, k_win, v_win):
    b, s, _, _ = q.shape
    n_cmp = k_cmp.shape[1]
    n_sel = k_blk.shape[2]
    n_top = min(SEL_TOPK, n_sel)
    G, R, L = NSA_KV_HEADS, NSA_GROUP, SEL_LEN
    cmp_start = jnp.arange(n_cmp) * CMP_STRIDE
    cmp_end = cmp_start + CMP_LEN - 1
    sel_start = jnp.arange(n_sel) * SEL_LEN
    ov = (jnp.minimum(cmp_end[:, None], sel_start[None, :] + L - 1)
          - jnp.maximum(cmp_start[:, None], sel_start[None, :]) + 1)
    overlap = jnp.maximum(ov, 0).astype(jnp.float32) / CMP_LEN
    bi = jnp.arange(b)[:, None, None, None]
    gi = jnp.arange(G)[None, :, None, None]
    jsel = jnp.arange(n_sel)

    def chunk(c):
        t0 = c * Q_CHUNK
        t = t0 + jnp.arange(Q_CHUNK)
        qc = lax.dynamic_slice_in_dim(q, t0, Q_CHUNK, axis=1).reshape(b, Q_CHUNK, G, R, HEAD_DIM)
        gc = lax.dynamic_slice_in_dim(gates, t0, Q_CHUNK, axis=1).reshape(b, Q_CHUNK, G, R, 3)
        sc = jnp.einsum('bqgrd,bngd->bgrqn', qc, k_cmp)
        p_cmp = masked_softmax(sc, cmp_end[None, :] <= t[:, None])
        o_cmp = jnp.einsum('bgrqn,bngd->bqgrd', p_cmp.astype(v_cmp.dtype), v_cmp)
        imp = jnp.einsum('bgrqn,nj->bgqj', p_cmp, overlap)
        cur = t // L
        forced = (jsel[None, :] == 0) | (jsel[None, :] == cur[:, None]) | (jsel[None, :] == cur[:, None] - 1)
        ok = sel_start[None, :] <= t[:, None]
        score = jnp.where(ok, jnp.where(forced, FORCE_SCORE, imp), NEG_INF)
        _, idx = lax.top_k(score, n_top)
        k_sel = k_blk[bi, gi, idx].reshape(b, G, Q_CHUNK, n_top * L, HEAD_DIM)
        v_sel = v_blk[bi, gi, idx].reshape(b, G, Q_CHUNK, n_top * L, HEAD_DIM)
        key_pos = (idx[..., None] * L + jnp.arange(L)).reshape(b, G, 1, Q_CHUNK, n_top * L)
        ss = jnp.einsum('bqgrd,bgqmd->bgrqm', qc, k_sel)
        p_sel = masked_softmax(ss, key_pos <= t[None, None, None, :, None])
        o_sel = jnp.einsum('bgrqm,bgqmd->bqgrd', p_sel.astype(v_sel.dtype), v_sel)
        kw = lax.dynamic_slice_in_dim(k_win, t0, WINDOW + Q_CHUNK, axis=1)
        vw = lax.dynamic_slice_in_dim(v_win, t0, WINDOW + Q_CHUNK, axis=1)
        spos = t0 - WINDOW + jnp.arange(WINDOW + Q_CHUNK)
        dist = t[:, None] - spos[None, :]
        win_ok = (dist >= 0) & (dist < WINDOW) & (spos[None, :] >= 0)
        sw = jnp.einsum('bqgrd,bsgd->bgrqs', qc, kw)
        p_w = masked_softmax(sw, win_ok)
        o_win = jnp.einsum('bgrqs,bsgd->bqgrd', p_w.astype(vw.dtype), vw)
        return gc[..., 0:1] * o_cmp + gc[..., 1:2] * o_sel + gc[..., 2:3] * o_win

    out = lax.map(chunk, jnp.arange(s // Q_CHUNK))
    return jnp.moveaxis(out, 0, 1).reshape(b, s, NSA_W)


def pool_layer(x, mem, norm_g, w_in, w_pool, pool_scale, mem_g, w_mem_kv, w_out):
    b, s, _ = x.shape
    h = rmsnorm(x, norm_g)
    proj = h @ w_in
    u, zu, qm, zm = jnp.split(proj, [POOL_W, 2 * POOL_W, 2 * POOL_W + MEM_W], axis=-1)
    y_pool = multiscale_pool(u, w_pool, pool_scale) * jax.nn.silu(zu)
    qm = qm.reshape(b, s, MEM_HEADS, HEAD_DIM) * (HEAD_DIM ** -0.5)
    y_mem = memory_attention(qm, mem, mem_g, w_mem_kv) * jax.nn.silu(zm)
    return x + jnp.concatenate([y_pool, y_mem], axis=-1) @ w_out


def nsa_layer(x, mem, positions, norm_g, w_in, mem_g, w_mem_kv, w_out, shared):
    b, s, _ = x.shape
    h = rmsnorm(x, norm_g)
    proj = h @ w_in
    o1 = NSA_W
    o2 = o1 + 3 * NSA_HEADS
    o3 = o2 + NSA_W
    o4 = o3 + MEM_W
    q, gl, zq, qm, zm = jnp.split(proj, [o1, o2, o3, o4], axis=-1)
    q = rope(q.reshape(b, s, NSA_HEADS, HEAD_DIM), positions) * (HEAD_DIM ** -0.5)
    gates = jax.nn.sigmoid(gl.reshape(b, s, NSA_HEADS, 3))
    y_nsa = nsa_attention(q, gates, *shared) * jax.nn.silu(zq)
    qm = qm.reshape(b, s, MEM_HEADS, HEAD_DIM) * (HEAD_DIM ** -0.5)
    y_mem = memory_attention(qm, mem, mem_g, w_mem_kv) * jax.nn.silu(zm)
    return x + jnp.concatenate([y_nsa, y_mem], axis=-1) @ w_out


def setup_inputs(seed: int = 0) -> dict:
    key = jax.random.key(seed)
    ks = jax.random.split(key, 20)
    n_a = DEPTH // 2
    n_b = DEPTH - n_a
    f32 = jnp.float32

    def w(k, shape, fan_in):
        return jax.random.normal(k, shape, f32) * (fan_in ** -0.5)

    def gain(k, shape):
        return 1.0 + 0.05 * jax.random.normal(k, shape, f32)

    x = jax.random.normal(ks[0], (BATCH, SEQ, D_MODEL), f32)
    mem = jax.random.normal(ks[1], (BATCH, MEM_LEN, D_MODEL), f32)
    offs = jax.random.randint(ks[2], (BATCH, 1), 0, 1024, dtype=jnp.int32)
    positions = (offs + jnp.arange(SEQ, dtype=jnp.int32)[None, :]).astype(jnp.int32)
    b_in_w = 2 * NSA_W + 3 * NSA_HEADS + 2 * MEM_W
    return {
        "x": x,
        "mem": mem,
        "positions": positions,
        "norm_g": gain(ks[3], (DEPTH, D_MODEL)),
        "mem_norm_g": gain(ks[4], (DEPTH, D_MODEL)),
        "w_mem_kv": w(ks[5], (DEPTH, D_MODEL, 2 * MEM_W), D_MODEL),
        "w_out": w(ks[6], (DEPTH, MIX_W, D_MODEL), MIX_W),
        "a_w_in": w(ks[7], (n_a, D_MODEL, 2 * POOL_W + 2 * MEM_W), D_MODEL),
        "a_w_pool": w(ks[8], (n_a, POOL_GROUPS, POOL_GC, POOL_GC), POOL_GC),
        "a_pool_scale": gain(ks[9], (n_a, POOL_W)),
        "b_w_in": w(ks[10], (n_b, D_MODEL, b_in_w), D_MODEL),
        "kv_norm_g": gain(ks[11], (D_MODEL,)),
        "w_kv": w(ks[12], (D_MODEL, 6 * NSA_KV_W), D_MODEL),
        "cmp_pe": 0.1 * jax.random.normal(ks[13], (2, CMP_LEN, HEAD_DIM), f32),
        "cmp_w1": w(ks[14], (2, CMP_LEN * HEAD_DIM, CMP_HID), CMP_LEN * HEAD_DIM),
        "cmp_w2": w(ks[15], (2, CMP_HID, HEAD_DIM), CMP_HID),
        "final_g": gain(ks[16], (D_MODEL,)),
    }


def reference(x, mem, positions, norm_g, mem_norm_g, w_mem_kv, w_out, a_w_in, a_w_pool,
              a_pool_scale, b_w_in, kv_norm_g, w_kv, cmp_pe, cmp_w1, cmp_w2, final_g):
    n_a = DEPTH // 2
    h = x
    shared = None
    for layer in range(DEPTH):
        if layer < n_a:
            h = pool_layer(h, mem, norm_g[layer], a_w_in[layer], a_w_pool[layer],
                           a_pool_scale[layer], mem_norm_g[layer], w_mem_kv[layer], w_out[layer])
        else:
            if layer == n_a:
                shared = nsa_shared_kv(h, kv_norm_g, w_kv, cmp_pe, cmp_w1, cmp_w2, positions)
            h = nsa_layer(h, mem, positions, norm_g[layer], b_w_in[layer - n_a],
                          mem_norm_g[layer], w_mem_kv[layer], w_out[layer], shared)
    return rmsnorm(h, final_g)
```

```python
import os
import numpy as np
import ml_dtypes
from contextlib import ExitStack
import concourse.bass as bass
import concourse.mybir as mybir
from concourse.bass_utils import run_bass_kernel_spmd

F32 = mybir.dt.float32
BF16 = mybir.dt.bfloat16
I32 = mybir.dt.int32
ALU = mybir.AluOpType
AF = mybir.ActivationFunctionType

ENGS = ("pe", "act", "dve", "pool", "sp")
NDMA_SEMS = 12


class Op:
    __slots__ = ("eng", "fn", "reads", "writes", "dma", "waits", "sig", "idx")

    def __init__(self, eng, fn, reads, writes, dma):
        self.eng, self.fn, self.reads, self.writes, self.dma = eng, fn, reads, writes, dma
        self.waits = []
        self.sig = None


class Prog:
    def __init__(self):
        self.ops = []
        self.dry = False

    def op(self, eng, fn, reads=(), writes=(), dma=False):
        if self.dry:
            return None
        o = Op(eng, fn, tuple(reads), tuple(writes), dma)
        self.ops.append(o)
        return o

    def barrier(self, skip=()):
        if not self.dry:
            self.ops.append(("bar", tuple(skip)))

    def analyze(self):
        last_w = {}
        readers = {}
        cnt = {e: 0 for e in ENGS}
        dma_n = {e: 0 for e in ENGS}
        dma_hist = {e: [] for e in ENGS}
        sigval = {}
        deps_of = []
        ops2 = []
        pend = {e: set() for e in ENGS}
        last_c = {}
        for o in self.ops:
            if isinstance(o, tuple):
                continue
            o.idx = len(ops2)
            ops2.append(o)
        ops_all = self.ops
        self.ops = ops2
        for o in ops_all:
            if isinstance(o, tuple):
                bd = set(last_c.values())
                for q in ENGS:
                    if q in o[1]:
                        continue
                    bd |= set(dma_hist[q][-NDMA_SEMS:])
                for e in ENGS:
                    pend[e] |= bd
                continue
            deps = set(pend[o.eng])
            pend[o.eng] = set()
            if not o.dma:
                last_c[o.eng] = o.idx
            for r in o.reads:
                deps.update(last_w.get(r, ()))
            for r in o.writes:
                deps.update(last_w.get(r, ()))
                for rd in readers.get(r, ()):
                    deps.add(rd)
            deps.discard(o.idx)
            keep = set()
            for d in deps:
                p = self.ops[d]
                if p.eng == o.eng and not p.dma and not o.dma:
                    if o.eng == "pe":
                        continue
                    raw = any((r in p.writes) for r in o.reads)
                    if not raw and o.eng != "pool":
                        continue
                keep.add(d)
            if o.dma:
                h = dma_hist[o.eng]
                if len(h) >= NDMA_SEMS:
                    keep.add(h[-NDMA_SEMS])
                h.append(o.idx)
            deps_of.append(keep)
            for r in o.writes:
                if o.dma:
                    last_w[r] = [w for w in last_w.get(r, ()) if self.ops[w].dma] + [o.idx]
                else:
                    last_w[r] = [o.idx]
                readers[r] = []
            for r in o.reads:
                if r not in o.writes:
                    lst = readers.setdefault(r, [])
                    if not o.dma:
                        lst[:] = [x for x in lst if self.ops[x].dma or self.ops[x].eng != o.eng]
                    lst.append(o.idx)
        needed = set()
        for k in deps_of:
            needed |= k
        for o in self.ops:
            if o.dma:
                m = dma_n[o.eng]
                dma_n[o.eng] += 1
                key = ("dma", o.eng, m % NDMA_SEMS)
                o.sig = (key, 16)
                sigval[o.idx] = (key, 16 * (m // NDMA_SEMS + 1))
            elif o.idx in needed:
                cnt[o.eng] += 1
                key = ("eng", o.eng)
                o.sig = (key, 1)
                sigval[o.idx] = (key, cnt[o.eng])
        waited = {e: {} for e in ENGS}
        for o in self.ops:
            best = {}
            for d in deps_of[o.idx]:
                key, v = sigval[d]
                if v > best.get(key, 0):
                    best[key] = v
            for key, v in best.items():
                if waited[o.eng].get(key, 0) >= v:
                    continue
                waited[o.eng][key] = v
                o.waits.append((key, v))
        self.final = {}
        for o in self.ops:
            if o.dma:
                key, v = sigval[o.idx]
                self.final[key] = max(self.final.get(key, 0), v)

    def emit(self, nc, es):
        self.analyze()
        sems = {}

        def sem(key):
            if key not in sems:
                sems[key] = es.enter_context(nc.semaphore("s_" + "_".join(str(k) for k in key)))
            return sems[key]

        for o in self.ops:
            if o.sig:
                sem(o.sig[0])
        per = {e: [o for o in self.ops if o.eng == e] for e in ENGS}
        block = es.enter_context(nc.Block())

        def run(engobj, name):
            for o in per[name]:
                for key, v in o.waits:
                    engobj.wait_ge(sem(key), v)
                ins = o.fn(engobj)
                if o.sig:
                    ins.then_inc(sem(o.sig[0]), o.sig[1])
            if name == "sp":
                for key, v in self.final.items():
                    engobj.wait_ge(sem(key), v)

        @block.tensor
        def _(e):
            run(e, "pe")

        @block.scalar
        def _(e):
            run(e, "act")

        @block.vector
        def _(e):
            run(e, "dve")

        @block.gpsimd
        def _(e):
            run(e, "pool")

        @block.sync
        def _(e):
            run(e, "sp")


D = 2048
S = 4096
HALF = 2048
MEM = 256
EPS = 1e-6
SCALE = 128.0 ** -0.5
NEG = -30000.0
WINS = (2, 4, 8, 16)
TWO_PI = 2.0 * np.pi


def bc(ap2d, reps):
    return bass.AP(ap2d.tensor, ap2d.offset, [list(ap2d.ap[0]), [0, reps], list(ap2d.ap[1])])


def build(dbg=()):
    nc = bass.Bass("TRN2", target_bir_lowering=False)
    P = Prog()

    def din(name, shape, dt=F32):
        return nc.dram_tensor(name, list(shape), dt, kind="ExternalInput").ap()

    def dscr(name, shape, dt):
        return nc.dram_tensor(name, list(shape), dt, kind="Internal").ap()

    x_d = din("x", [S, D])
    pos_d = din("pos", [1, S], I32)
    mem_d = din("mem", [MEM, D])
    wf = {
        "w_in0": din("w_in0", [D, 4096]), "w_in1": din("w_in1", [D, 4096]),
        "w_out0": din("w_out0", [D, D]), "w_out1": din("w_out1", [D, D]),
        "w_kv": din("w_kv", [D, 3072]), "w_gl": din("w_gl", [D, 36]),
        "w_mkv0": din("w_mkv0", [D, 1024]), "w_mkv1": din("w_mkv1", [D, 1024]),
        "w_pool": din("w_pool", [1536, 384]), "cw1": din("cw1", [8192, 256]),
        "cw2": din("cw2", [512, 128]),
    }
    wb = {k: dscr(k + "_b", v.shape, BF16) for k, v in wf.items()}
    gT_d = din("gT", [128, 5, 16])
    gfin_d = din("gfin", [1, D])
    pscale_d = din("pscale", [128, 12])
    peT_d = din("peT", [128, 2, 32])
    cpool_d = din("cpool", [128, 2, 4, 16])
    hflag_d = din("hflag", [128, 1])
    rope_d = din("ropec", [32, 4])
    ident_d = din("ident", [128, 128], BF16)
    rperm_d = din("rperm", [128, 128], BF16)
    M1_d = din("M1", [HALF, 64])
    C_d = din("Ctab", [HALF, 64])
    CB_d = din("CB", [2, 128, HALF], BF16)
    E_d = din("Eall", [128, S], BF16)
    WB_d = din("WB", [128, 4, 128], BF16)
    ovl_d = din("ovl", [2, 128, 64], BF16)

    h1_d = dscr("h1_s", [HALF, D], F32)
    kcT_d = dscr("kcT_s", [4, 128, S], BF16)
    vcT_d = dscr("vcT_s", [4, 128, S], BF16)
    ksT_d = dscr("ksT_s", [4, 128, S], BF16)
    kwT_d = dscr("kwT_s", [4, 128, S], BF16)
    vs_d = dscr("vs_s", [S, 512], BF16)
    vw_d = dscr("vw_s", [S, 512], BF16)
    cs_d = dscr("cs_s", [2, 32, S], BF16)

    DN = {id(kcT_d): "kcT_d", id(vcT_d): "vcT_d", id(ksT_d): "ksT_d", id(kwT_d): "kwT_d", id(vs_d): "vs_d", id(vw_d): "vw_d"}
    out_d = nc.dram_tensor("out", [HALF, D], F32, kind="ExternalOutput").ap()
    dbg_out = {}
    if "h1" in dbg:
        dbg_out["h1"] = nc.dram_tensor("dbg_h1", [HALF, D], F32, kind="ExternalOutput").ap()
    if "kv" in dbg:
        dbg_out["ksT"] = nc.dram_tensor("dbg_ksT", [4, 128, S], BF16, kind="ExternalOutput").ap()
        dbg_out["kcT"] = nc.dram_tensor("dbg_kcT", [4, 128, S], BF16, kind="ExternalOutput").ap()
        dbg_out["vs"] = nc.dram_tensor("dbg_vs", [S, 512], BF16, kind="ExternalOutput").ap()
        dbg_out["kwT"] = nc.dram_tensor("dbg_kwT", [4, 128, S], BF16, kind="ExternalOutput").ap()

    with ExitStack() as es:
        def sb(n, s, d):
            return es.enter_context(nc.sbuf_tensor("sb_" + n, list(s), d))

        def ps(n, s, d):
            return es.enter_context(nc.psum_tensor("ps_" + n, list(s), d))

        ident = sb("ident", [128, 128], BF16)
        ones = sb("ones", [128, 128], BF16)
        rperm = sb("rperm", [128, 128], BF16)
        gT = sb("gT", [128, 5, 16], F32)
        pscale = sb("pscale", [128, 12], F32)
        cpool = sb("cpool", [128, 2, 4, 16], F32)
        hflag = sb("hflag", [128, 1], F32)
        ropec = sb("ropec", [32, 4], F32)
        epsb = sb("epsb", [128, 1], F32)
        ss = sb("ss", [128, 8], F32)
        rs = sb("rs", [128, 8], F32)
        wbuf = [sb("wbuf%d" % i, [128, 16, 512], BF16) for i in range(2)]
        mkT = sb("mkT", [128, 4, 256], BF16)
        mv = sb("mv", [128, 2, 512], BF16)
        kcmpT = sb("kcmpT", [128, 4, 256], BF16)
        vcmpx = sb("vcmpx", [128, 2, 4, 194], BF16)
        FA = sb("FA", [128, 19456], F32)
        BA = sb("BA", [128, 36864], BF16)
        pA = [ps("pA%d" % i, [128, 512], F32) for i in range(4)]
        pT = [ps("pT%d" % i, [128, 1024], BF16) for i in range(2)]
        pO = ps("pO", [128, 1024], F32)

        class Rot:
            def __init__(self, items):
                self.items, self.i = items, 0

            def next(self):
                it = self.items[self.i % len(self.items)]
                self.i += 1
                return it

        PA = Rot([(pA[i], "pA%d" % i) for i in range(4)])
        PT = Rot([(pT[i], "pT%d" % i) for i in range(2)])

        class WS:
            def __init__(self):
                self.sched, self.n, self.loaded = [], 0, 0

            def _load(self, k):
                name, cb = self.sched[k]
                slot = wbuf[k % 2]
                src = wb[name][:, cb * 512:(cb + 1) * 512].rearrange("(kc p) n -> p kc n", p=128)
                P.op("sp", lambda e: e.dma_start(out=slot[:], in_=src), reads=["%s_b%d" % (name, cb)],
                     writes=["wbuf%d" % (k % 2)], dma=True)

            def next(self, name, cb):
                k = self.n
                self.n += 1
                if P.dry:
                    self.sched.append((name, cb))
                    return wbuf[k % 2], "wbuf%d" % (k % 2)
                assert self.sched[k] == (name, cb)
                while self.loaded <= min(k + 1, len(self.sched) - 1):
                    self._load(self.loaded)
                    self.loaded += 1
                return wbuf[k % 2], "wbuf%d" % (k % 2)

        W = WS()

        class Stream:
            def __init__(self, nbuf, look):
                self.sched, self.n, self.loaded, self.nbuf, self.look = [], 0, 0, nbuf, look

            def next(self, loader):
                k = self.n
                self.n += 1
                if P.dry:
                    self.sched.append(loader)
                    return k % self.nbuf
                while self.loaded <= min(k + self.look, len(self.sched) - 1):
                    self.sched[self.loaded](self.loaded % self.nbuf)
                    self.loaded += 1
                return k % self.nbuf

        KVS = Stream(4, 2)
        KVW = Stream(2, 1)

        def evac(kind, out, in_, reads, writes, eng=None, scale=None):
            if kind == "copy":
                if eng == "dve":
                    P.op("dve", lambda e: e.tensor_copy(out=out, in_=in_), reads, writes)
                else:
                    P.op("act", lambda e: e.copy(out=out, in_=in_), reads, writes)
            elif kind == "silu":
                P.op("act", lambda e: e.activation(out=out, in_=in_, func=AF.Silu), reads, writes)
            elif kind == "exp":
                P.op("act", lambda e: e.activation(out=out, in_=in_, func=AF.Exp, scale=scale), reads, writes)

        def mm(out, lhsT, rhs, start, stop, reads, writes):
            P.op("pe", lambda e: e.matmul(out, lhsT=lhsT, rhs=rhs, start=start, stop=stop), reads, writes)

        xt = FA[:, 0:8192].rearrange("p (s n) -> p s n", s=4)
        uT = FA[:, 8192:8192 + 12 * 528].rearrange("p (c n) -> p c n", c=12)
        o = 8192 + 12 * 528
        ta = FA[:, o:o + 1584].rearrange("p (c n) -> p c n", c=3)
        tb = FA[:, o + 1584:o + 3168].rearrange("p (c n) -> p c n", c=3)
        o += 3168
        rzb = FA[:, o:o + 512]
        tm2 = FA[:, o + 512:o + 1024]
        o += 1024
        rtmp = FA[:, o:o + 256]
        hb = BA[:, 0:8192].rearrange("p (s n) -> p s n", s=4)
        yT = BA[:, 0:8192].rearrange("p (c n) -> p c n", c=16)
        hT = BA[:, 8192:16384].rearrange("p (c n) -> p c n", c=16)
        szu = BA[:, 16384:16384 + 6144].rearrange("p (c n) -> p c n", c=12)
        qmT = BA[:, 22528:22528 + 2048].rearrange("p (c n) -> p c n", c=4)
        szm = BA[:, 24576:24576 + 2048].rearrange("p (c n) -> p c n", c=4)
        poolb = [BA[:, 26624 + i * 1536:26624 + (i + 1) * 1536].rearrange("p (c n) -> p c n", c=3) for i in range(2)]
        pexp = BA[:, 29696:29696 + 1024].rearrange("p (c n) -> p c n", c=2)
        kst = [BA[:, 30720 + i * 512:30720 + (i + 1) * 512] for i in range(3)]
        cst = BA[:, 32256:32256 + 1024].rearrange("p (c n) -> p c n", c=2)
        wpool = sb("wpool", [128, 12, 384], BF16)
        posi_t = sb("posi", [32, 1024], I32)
        posn_t = [sb("posn0", [32, 1024], I32)] * 2
        KST = Rot([(kst[i], "kst%d" % i) for i in range(3)])

        def setup():
            order = [("w_mkv0", (0, 1)), ("w_in0", (3, 4, 5, 0, 1, 2, 6, 7)), ("w_pool", None), ("w_out0", (0, 1, 2, 3)),
                     ("w_kv", (0, 1, 2, 4, 3, 5)), ("cw1", None), ("cw2", None), ("w_mkv1", (0, 1)),
                     ("w_in1", (0, 1, 2, 3, 4, 5, 6, 7)), ("w_gl", None), ("w_out1", (0, 1, 2, 3))]
            for k, cbs in order:
                src, dst = wf[k], wb[k]
                if cbs is None:
                    P.op("pool", lambda e, s=src, d=dst: e.dma_start(out=d, in_=s), reads=[], writes=[k + "_b"], dma=True)
                else:
                    for cb in cbs:
                        P.op("pool", lambda e, s=src[:, cb * 512:(cb + 1) * 512], d=dst[:, cb * 512:(cb + 1) * 512]: e.dma_start(out=d, in_=s),
                             reads=[], writes=["%s_b%d" % (k, cb)], dma=True)
            for t, d, nm in ((ident, ident_d, "ident"), (rperm, rperm_d, "rperm"), (gT, gT_d, "gT"),
                             (pscale, pscale_d, "pscale"), (cpool, cpool_d, "cpool"), (hflag, hflag_d, "hflag"),
                             (ropec, rope_d, "ropec")):
                P.op("sp", lambda e, t=t, d=d: e.dma_start(out=t[:], in_=d), writes=[nm], dma=True)
            P.op("sp", lambda e: e.dma_start(out=wpool[:], in_=wb["w_pool"].rearrange("(g p) n -> p g n", p=128)),
                 reads=["w_pool_b"], writes=["wpool"], dma=True)
            P.op("pool", lambda e: e.memset(ones[:], 1.0), writes=["ones"])
            P.op("pool", lambda e: e.memset(epsb[:], EPS), writes=["epsb"])
            for pc in range(4):
                pi = posi_t
                pf = FA[0:32, 1024 * pc:1024 * (pc + 1)]
                P.op("sp", lambda e, pi=pi, pc=pc: e.dma_start(out=pi[:], in_=pos_d[:, pc * 1024:(pc + 1) * 1024].to_broadcast([32, 1024])),
                     writes=["posi"], dma=True)
                P.op("dve", lambda e, pi=pi, pf=pf: e.tensor_copy(out=pf, in_=pi[:]), reads=["posi"], writes=["pf%d" % pc])
                for kind in range(2):
                    ang = FA[0:32, 4096 + 1024 * kind:4096 + 1024 * (kind + 1)]
                    nf = FA[0:32, 6144 + 1024 * kind:6144 + 1024 * (kind + 1)]
                    ni = posn_t[kind]
                    tab = BA[0:32, 1024 * kind:1024 * (kind + 1)]
                    A, N_, NI = "ang%d" % kind, "nf%d" % kind, "ni"
                    P.op("dve", lambda e, ang=ang, pf=pf, kind=kind: e.tensor_scalar(
                        out=ang, in0=pf, scalar1=ropec[:, 0:1], scalar2=ropec[:, 1 + kind:2 + kind], op0=ALU.mult, op1=ALU.add),
                        reads=["pf%d" % pc, "ropec"], writes=[A])
                    P.op("dve", lambda e, ang=ang, ni=ni: e.tensor_scalar(
                        out=ni[:], in0=ang, scalar1=1.0 / TWO_PI, scalar2=None, op0=ALU.mult), reads=[A], writes=[NI])
                    P.op("dve", lambda e, nf=nf, ni=ni: e.tensor_copy(out=nf, in_=ni[:]), reads=[NI], writes=[N_])
                    P.op("dve", lambda e, ang=ang, nf=nf: e.scalar_tensor_tensor(
                        out=ang, in0=nf, scalar=-TWO_PI, in1=ang, op0=ALU.mult, op1=ALU.add), reads=[N_, A], writes=[A])
                    P.op("dve", lambda e, ang=ang, nf=nf: e.tensor_scalar(
                        out=nf, in0=ang, scalar1=float(np.pi), scalar2=-TWO_PI, op0=ALU.is_gt, op1=ALU.mult), reads=[A], writes=[N_])
                    P.op("dve", lambda e, ang=ang, nf=nf: e.tensor_tensor(out=ang, in0=ang, in1=nf, op=ALU.add), reads=[A, N_], writes=[A])
                    P.op("act", lambda e, ang=ang, tab=tab: e.activation(out=tab, in_=ang, func=AF.Sin),
                         reads=[A], writes=["tab%d" % kind])
                    P.op("sp", lambda e, tab=tab, kind=kind, pc=pc: e.dma_start(out=cs_d[kind, :, pc * 1024:(pc + 1) * 1024], in_=tab),
                         reads=["tab%d" % kind], writes=["cs_d"], dma=True)

        def rmsnorm_T(gidx, ntok_sub, src_res):
            for s in range(ntok_sub):
                P.op("act", lambda e, s=s: e.activation(out=hb[:, s, :], in_=xt[:, s, :], func=AF.Square, accum_out=ss[:, s:s + 1]),
                     reads=["xt%d" % s], writes=["HY", "ss%d" % s])
                P.op("act", lambda e, s=s: e.activation(out=rs[:, s:s + 1], in_=ss[:, s:s + 1], func=AF.Sqrt, scale=1.0 / D, bias=epsb[:]),
                     reads=["ss%d" % s, "epsb"], writes=["rs%d" % s])
                P.op("dve", lambda e, s=s: e.reciprocal(out=rs[:, s:s + 1], in_=rs[:, s:s + 1]), reads=["rs%d" % s], writes=["rs%d" % s])
                P.op("dve", lambda e, s=s: e.tensor_scalar(out=hb[:, s, :], in0=xt[:, s, :], scalar1=rs[:, s:s + 1], scalar2=None, op0=ALU.mult),
                     reads=["xt%d" % s, "rs%d" % s], writes=["HY"])
            n = ntok_sub * 128
            for c in range(16):
                bank, bres = PT.next()
                for s in range(ntok_sub):
                    P.op("pe", lambda e, s=s, c=c, bank=bank: e.transpose(out=bank[:, s * 128:(s + 1) * 128], in_=hb[:, s, c * 128:(c + 1) * 128], identity=ident[:]),
                         reads=["HY", "ident"], writes=[bres])
                if c % 2 == 0:
                    P.op("act", lambda e, c=c, bank=bank: e.activation(out=hT[:, c, 0:n], in_=bank[:, 0:n], func=AF.Copy, scale=gT[:, gidx, c:c + 1]),
                         reads=[bres, "gT"], writes=["hT"])
                else:
                    P.op("dve", lambda e, c=c, bank=bank: e.tensor_scalar(out=hT[:, c, 0:n], in0=bank[:, 0:n], scalar1=gT[:, gidx, c:c + 1], scalar2=None, op0=ALU.mult),
                         reads=[bres, "gT"], writes=["hT"])

        def proj_fm(wname, cb, n, sink):
            slot, sres = W.next(wname, cb)
            for j in range(4):
                pa, pres = PA.next()
                for kc in range(16):
                    mm(pa[:, 0:n], slot[:, kc, j * 128:(j + 1) * 128], hT[:, kc, 0:n], kc == 0, kc == 15, [sres, "hT"], [pres])
                sink(j, pa, pres)

        def mem_kv(layer):
            for s in range(2):
                P.op("sp", lambda e, s=s: e.dma_start(out=xt[:, s, :], in_=mem_d[s * 128:(s + 1) * 128, :]), writes=["xt%d" % s], dma=True)
            rmsnorm_T(3 + layer, 2, None)
            wname = "w_mkv%d" % layer

            def sink_k(j, pa, pres):
                evac("copy", mkT[:, j, :], pa[:, 0:256], [pres], ["mkT"], eng="dve" if j % 2 else None)
            proj_fm(wname, 0, 256, sink_k)
            slot, sres = W.next(wname, 1)
            for mc in range(2):
                pa, pres = PA.next()
                for kc in range(16):
                    mm(pa[:, :], hT[:, kc, mc * 128:(mc + 1) * 128], slot[:, kc, :], kc == 0, kc == 15, [sres, "hT"], [pres])
                evac("copy", mv[:, mc, :], pa[:, :], [pres], ["mv"])

        def mem_attn(n, ybase):
            for h in range(4):
                for mc in range(2):
                    pa, pres = PA.next()
                    mm(pa[:, 0:n], mkT[:, h, mc * 128:(mc + 1) * 128], qmT[:, h, 0:n], True, True, ["mkT", "qmT"], [pres])
                    evac("exp", pexp[:, mc, 0:n], pa[:, 0:n], [pres], ["pexp%d" % mc], scale=SCALE)
                po, pores = PA.next()
                for mc in range(2):
                    mm(po[:, 0:n], mv[:, mc, h * 128:(h + 1) * 128], pexp[:, mc, 0:n], mc == 0, mc == 1, ["mv", "pexp%d" % mc], [pores])
                pz, pzres = PA.next()
                for mc in range(2):
                    mm(pz[:, 0:n], ones[:], pexp[:, mc, 0:n], mc == 0, mc == 1, ["ones", "pexp%d" % mc], [pzres])
                P.op("dve", lambda e, pz=pz: e.reciprocal(out=rzb[:, 0:n], in_=pz[:, 0:n]), reads=[pzres], writes=["rzb"])
                P.op("dve", lambda e, po=po: e.tensor_tensor(out=tm2[:, 0:n], in0=po[:, 0:n], in1=rzb[:, 0:n], op=ALU.mult), reads=[pores, "rzb"], writes=["tm2"])
                P.op("dve", lambda e, h=h: e.tensor_tensor(out=yT[:, ybase + h, 0:n], in0=tm2[:, 0:n], in1=szm[:, h, 0:n], op=ALU.mult),
                     reads=["tm2", "szm"], writes=["HY"])

        def out_proj(wname, nsub, n_cb=4):
            for cb in range(n_cb):
                slot, sres = W.next(wname, cb)
                for s in range(nsub):
                    pa, pres = PA.next()
                    for kc in range(16):
                        mm(pa[:, :], yT[:, kc, s * 128:(s + 1) * 128], slot[:, kc, :], kc == 0, kc == 15, [sres, "HY"], [pres])
                    P.op("dve", lambda e, s=s, cb=cb, pa=pa: e.tensor_tensor(out=xt[:, s, cb * 512:(cb + 1) * 512], in0=pa[:, :], in1=xt[:, s, cb * 512:(cb + 1) * 512], op=ALU.add),
                         reads=[pres, "xt%d" % s], writes=["xt%d" % s])

        def layer0_tile(tt):
            T0 = tt * 512
            for s in range(4):
                P.op("sp", lambda e, s=s: e.dma_start(out=xt[:, s, :], in_=x_d[T0 + s * 128:T0 + (s + 1) * 128, :]), writes=["xt%d" % s], dma=True)
            P.op("sp", lambda e: e.dma_start(out=cst[0:32, :, :], in_=cs_d[:, :, T0:T0 + 512].rearrange("k p n -> p k n")), reads=["cs_d"], writes=["cst"], dma=True)
            rmsnorm_T(0, 4, None)
            if tt == 0:
                P.op("pool", lambda e: e.memset(uT[:, :, 0:16], 0.0), writes=["uT"])
            elif tt == 4:
                P.op("pool", lambda e: e.tensor_scalar(out=uT[:, :, 0:16], in0=uT[:, :, 512:528], scalar1=hflag[:, 0:1], scalar2=None, op0=ALU.mult),
                     reads=["uT", "hflag"], writes=["uT"])
            else:
                P.op("pool", lambda e: e.tensor_copy(out=uT[:, :, 0:16], in_=uT[:, :, 512:528]), reads=["uT"], writes=["uT"])

            def sink_zu(cb):
                def f(j, pa, pres):
                    c = (cb - 3) * 4 + j
                    evac("silu", szu[:, c, :], pa[:, :], [pres], ["szu"])
                return f

            def sink_u(cb):
                def f(j, pa, pres):
                    c = cb * 4 + j
                    evac("copy", uT[:, c, 16:528], pa[:, :], [pres], ["uT"], eng="dve" if j % 2 else None)
                return f

            def sink_qm(j, pa, pres):
                evac("copy", qmT[:, j, :], pa[:, :], [pres], ["qmT"], eng="dve")

            def sink_zm(j, pa, pres):
                evac("silu", szm[:, j, :], pa[:, :], [pres], ["szm"])

            for cb in (3, 4, 5):
                proj_fm("w_in0", cb, 512, sink_zu(cb))
            for cb in (0, 1, 2):
                proj_fm("w_in0", cb, 512, sink_u(cb))
            proj_fm("w_in0", 6, 512, sink_qm)
            proj_fm("w_in0", 7, 512, sink_zm)
            for g in range(4):
                U = uT[:, 3 * g:3 * g + 3, :]
                win = WINS[g]
                cur = U
                curres = "uT"
                lvl = 0
                sh = 1
                bufs = [(ta, "ta"), (tb, "tb")]
                while sh < win:
                    dst, dres = bufs[lvl % 2]
                    lo = 2 * sh
                    P.op("pool", lambda e, dst=dst, cur=cur, lo=lo, sh=sh: e.tensor_tensor(out=dst[:, :, lo:528], in0=cur[:, :, lo:528], in1=cur[:, :, lo - sh:528 - sh], op=ALU.add),
                         reads=[curres], writes=[dres])
                    cur, curres = dst, dres
                    lvl += 1
                    sh *= 2
                pb = poolb[g % 2]
                pbres = "poolb%d" % (g % 2)
                P.op("dve", lambda e, pb=pb, cur=cur, U=U, win=win: e.scalar_tensor_tensor(out=pb[:, :, :], in0=cur[:, :, 16:528], scalar=1.0 / win, in1=U[:, :, 16:528], op0=ALU.mult, op1=ALU.subtract),
                     reads=[curres, "uT"], writes=[pbres])
                if tt in (0, 4):
                    kind = 0 if tt == 0 else 1
                    ic = bass.AP(cpool[:, kind, g, :].tensor, cpool[:, kind, g, :].offset, [list(cpool[:, kind, g, :].ap[0]), [0, 3], [1, 16]])
                    P.op("dve", lambda e, cur=cur, ic=ic: e.tensor_tensor(out=cur[:, :, 0:16], in0=cur[:, :, 16:32], in1=ic, op=ALU.mult),
                         reads=[curres, "cpool"], writes=[curres])
                    P.op("dve", lambda e, cur=cur, pb=pb, U=U: e.tensor_tensor(out=pb[:, :, 0:16], in0=cur[:, :, 0:16], in1=U[:, :, 16:32], op=ALU.subtract),
                         reads=[curres, "uT"], writes=[pbres])
                for oc in range(3):
                    pa, pres = PA.next()
                    for kc in range(3):
                        mm(pa[:, :], wpool[:, g * 3 + kc, oc * 128:(oc + 1) * 128], pb[:, kc, :], kc == 0, kc == 2, ["wpool", pbres], [pres])
                    c = g * 3 + oc
                    P.op("dve", lambda e, pa=pa, c=c: e.scalar_tensor_tensor(out=yT[:, c, :], in0=pa[:, :], scalar=pscale[:, c:c + 1], in1=szu[:, c, :], op0=ALU.mult, op1=ALU.mult),
                         reads=[pres, "pscale", "szu"], writes=["HY"])
            mem_attn(512, 12)
            out_proj("w_out0", 4)
            if tt >= 4:
                for s in range(4):
                    r0 = (tt - 4) * 512 + s * 128
                    P.op("sp", lambda e, s=s, r0=r0: e.dma_start(out=h1_d[r0:r0 + 128, :], in_=xt[:, s, :]), reads=["xt%d" % s], writes=["h1_d"], dma=True)
                    if "h1" in dbg:
                        P.op("pool", lambda e, s=s, r0=r0: e.dma_start(out=dbg_out["h1"][r0:r0 + 128, :], in_=xt[:, s, :]), reads=["xt%d" % s], writes=["dbg_h1"], dma=True)
            rmsnorm_T(2, 4, None)

            def sink_plain(dst_d):
                def f(j, pa, pres):
                    st, stres = KST.next()
                    evac("copy", st[:, :], pa[:, :], [pres], [stres], eng="dve" if j % 2 else None)
                    P.op("sp", lambda e, st=st, j=j: e.dma_start(out=dst_d[j, :, T0:T0 + 512], in_=st[:, :]), reads=[stres], writes=[DN[id(dst_d)]], dma=True)
                return f

            def sink_rope(dst_d):
                def f(j, pa, pres):
                    st, stres = KST.next()
                    evac("copy", st[:, :], pa[:, :], [pres], [stres])
                    pr, prres = PA.next()
                    mm(pr[:, :], rperm[:, :], st[:, :], True, True, ["rperm", stres], [prres])
                    r1 = FA[0:32, 18720:18720 + 512]
                    r2 = FA[0:32, 17696:17696 + 512]
                    P.op("dve", lambda e, st=st, r1=r1: e.tensor_tensor(out=r1, in0=st[0:32, :], in1=cst[0:32, 0, :], op=ALU.mult), reads=[stres, "cst"], writes=["r1"])
                    P.op("dve", lambda e, pr=pr, r2=r2: e.tensor_tensor(out=r2, in0=pr[0:32, :], in1=cst[0:32, 1, :], op=ALU.mult), reads=[prres, "cst"], writes=["rzb"])
                    P.op("dve", lambda e, st=st, r1=r1, r2=r2: e.tensor_tensor(out=st[0:32, :], in0=r1, in1=r2, op=ALU.add), reads=["r1", "rzb"], writes=[stres])
                    P.op("sp", lambda e, st=st, j=j: e.dma_start(out=dst_d[j, :, T0:T0 + 512], in_=st[:, :]), reads=[stres], writes=[DN[id(dst_d)]], dma=True)
                return f

            proj_fm("w_kv", 0, 512, sink_plain(kcT_d))
            proj_fm("w_kv", 1, 512, sink_plain(vcT_d))
            proj_fm("w_kv", 2, 512, sink_rope(ksT_d))
            proj_fm("w_kv", 4, 512, sink_rope(kwT_d))
            for cb, dst_d in ((3, vs_d), (5, vw_d)):
                slot, sres = W.next("w_kv", cb)
                for s in range(4):
                    pa, pres = PA.next()
                    for kc in range(16):
                        mm(pa[:, :], hT[:, kc, s * 128:(s + 1) * 128], slot[:, kc, :], kc == 0, kc == 15, [sres, "hT"], [pres])
                    st, stres = KST.next()
                    evac("copy", st[:, :], pa[:, :], [pres], [stres], eng="dve" if s % 2 else None)
                    P.op("sp", lambda e, st=st, s=s, dst_d=dst_d: e.dma_start(out=dst_d[T0 + s * 128:T0 + (s + 1) * 128, :], in_=st[:, :]), reads=[stres], writes=[DN[id(dst_d)]], dma=True)

        def compress():
            cstab = BA[0:32, 16448:16448 + 8192].rearrange("p (k n) -> p k n", k=2)
            P.op("sp", lambda e: e.dma_start(out=cstab, in_=cs_d.rearrange("k p n -> p k n")), reads=["cs_d"], writes=["cstab"], dma=True)
            peb = BA[:, 24640:24640 + 68].rearrange("p (k l) -> p k l", k=2)
            pef = FA[:, 0:64].rearrange("p (k l) -> p k l", k=2)
            P.op("sp", lambda e: e.dma_start(out=pef, in_=peT_d), writes=["pef"], dma=True)
            P.op("dve", lambda e: e.memset(peb, 0.0), writes=["peb"])
            P.op("dve", lambda e: e.tensor_copy(out=peb[:, :, 0:32], in_=pef), reads=["pef", "peb"], writes=["peb"])
            bh = FA[:, 64:68].rearrange("p (k h) -> p k h", k=2)
            P.op("pool", lambda e: e.memset(kcmpT[:], 0.0), writes=["kcmpT"])
            P.op("pool", lambda e: e.memset(vcmpx[:], 0.0), writes=["vcmpx"])
            P.op("pool", lambda e: e.memset(vcmpx[:].rearrange("p a g d -> p (a g) d")[:, :, 192:193], 1.0), reads=["vcmpx"], writes=["vcmpx"])
            for nc_ in range(2):
                for g in range(4):
                    P.op("sp", lambda e, nc_=nc_, g=g: e.dma_start(out=vcmpx[:, nc_, g, 128:192], in_=ovl_d[nc_]), reads=["vcmpx"], writes=["vcmpx"], dma=True)
            zT = [BA[:, 0:4128], BA[:, 4128:8256]]
            w1s = BA[:, 8256:16448].rearrange("p (l h) -> p l h", l=32)
            w2s = BA[:, 24708:24708 + 256].rearrange("p (c d) -> p c d", c=2)
            hid = BA[:, 24964:24964 + 512].rearrange("p (c n) -> p c n", c=2)
            kst0 = BA[:, 25476:25476 + 256]
            for i_ in range(2):
                P.op("pool", lambda e, i_=i_: e.memset(zT[i_][:, 4096:4128], 0.0), writes=["zT%d" % i_])
            zi = 0
            for kv in range(2):
                P.op("sp", lambda e, kv=kv: e.dma_start(out=w1s, in_=wb["cw1"][kv * 4096:(kv + 1) * 4096, :].rearrange("(l p) h -> p l h", p=128)),
                     reads=["cw1_b"], writes=["w1s"], dma=True)
                P.op("sp", lambda e, kv=kv: e.dma_start(out=w2s, in_=wb["cw2"][kv * 256:(kv + 1) * 256, :].rearrange("(c p) d -> p c d", p=128)),
                     reads=["cw2_b"], writes=["w2s"], dma=True)
                for hc in range(2):
                    pa, pres = PA.next()
                    for l in range(32):
                        mm(pa[:, 0:2], w1s[:, l, hc * 128:(hc + 1) * 128], peb[:, kv, l:l + 2], l == 0, l == 31, ["w1s", "peb"], [pres])
                    P.op("dve", lambda e, pa=pa, kv=kv, hc=hc: e.tensor_copy(out=bh[:, kv, hc:hc + 1], in_=pa[:, 0:1]), reads=[pres], writes=["bh"])
                src_d = kcT_d if kv == 0 else vcT_d
                for g in range(4):
                    z = zT[zi % 2]
                    zres = "zT%d" % (zi % 2)
                    zi += 1
                    P.op("sp", lambda e, z=z, g=g, src_d=src_d: e.dma_start(out=z[:, 0:4096], in_=src_d[g]), reads=[DN[id(src_d)]], writes=[zres], dma=True)
                    for hc in range(2):
                        pa, pres = PA.next()
                        for l in range(32):
                            mm(pa[:, 0:256], w1s[:, l, hc * 128:(hc + 1) * 128], z[:, l:l + 16 * 255 + 1:16], l == 0, l == 31, ["w1s", zres], [pres])
                        P.op("act", lambda e, pa=pa, kv=kv, hc=hc: e.activation(out=hid[:, hc, 0:256], in_=pa[:, 0:256], func=AF.Silu, bias=bh[:, kv, hc:hc + 1]),
                             reads=[pres, "bh"], writes=["hid"])
                    if kv == 0:
                        pa, pres = PA.next()
                        for hc in range(2):
                            mm(pa[:, 0:256], w2s[:, hc, :], hid[:, hc, 0:256], hc == 0, hc == 1, ["w2s", "hid"], [pres])
                        evac("copy", kst0[:, 0:256], pa[:, 0:256], [pres], ["kst0"])
                        pr, prres = PA.next()
                        mm(pr[:, 0:256], rperm[:, :], kst0[:, 0:256], True, True, ["rperm", "kst0"], [prres])
                        r1 = FA[0:32, 18720:18720 + 255]
                        r2 = FA[0:32, 17696:17696 + 255]
                        P.op("dve", lambda e, r1=r1: e.tensor_tensor(out=r1, in0=kst0[0:32, 0:255], in1=cstab[:, 0, 31:31 + 16 * 254 + 1:16], op=ALU.mult), reads=["kst0", "cstab"], writes=["r1"])
                        P.op("dve", lambda e, pr=pr, r2=r2: e.tensor_tensor(out=r2, in0=pr[0:32, 0:255], in1=cstab[:, 1, 31:31 + 16 * 254 + 1:16], op=ALU.mult), reads=[prres, "cstab"], writes=["rzb"])
                        P.op("dve", lambda e, r1=r1, r2=r2: e.tensor_tensor(out=kst0[0:32, 0:255], in0=r1, in1=r2, op=ALU.add), reads=["r1", "rzb"], writes=["kst0"])
                        P.op("dve", lambda e, g=g: e.tensor_copy(out=kcmpT[:, g, 0:256], in_=kst0[:, 0:256]), reads=["kst0"], writes=["kcmpT"])
                    else:
                        for nc_ in range(2):
                            nn = 128
                            pa, pres = PA.next()
                            for hc in range(2):
                                mm(pa[0:nn, 0:128], hid[:, hc, nc_ * 128:nc_ * 128 + nn], w2s[:, hc, :], hc == 0, hc == 1, ["hid", "w2s"], [pres])
                            evac("copy", vcmpx[0:nn, nc_, g, 0:128], pa[0:nn, 0:128], [pres], ["vcmpx"], eng="dve" if nc_ else None)

        FB1 = FA[:, 4096:8192].bitcast(BF16)
        FB2 = FA[:, 13440:17696].bitcast(BF16)
        Es = FB1[:, 0:4096]
        ksg = [FB1[:, 4096 + i * 512:4096 + (i + 1) * 512] for i in range(2)]
        vsg = [FB1[:, 5120 + i * 520:5120 + (i + 1) * 520].rearrange("p (c d) -> p c d", c=4) for i in range(2)]
        ksg.append(BA[:, 33280:33792])
        vsg.append(BA[:, 33792:34312].rearrange("p (c d) -> p c d", c=4))
        ksg.append(BA[:, 35602:36114])
        vsg.append(BA[:, 36114:36634].rearrange("p (c d) -> p c d", c=4))
        kwg = [FB1[:, 6160:6800], BA[:, 34312:34952]]
        vwg = [FB1[:, 6800:7450].rearrange("p (c d) -> p c d", c=5), BA[:, 34952:35602].rearrange("p (c d) -> p c d", c=5)]
        selT = FB1[:, 7456:7840]
        ybf = FB2[:, 0:1536]
        pcm = [FB2[:, 1536 + i * 384:1536 + (i + 1) * 384] for i in range(2)]
        pse = [FB2[:, 2304 + i * 384:2304 + (i + 1) * 384] for i in range(3)]
        CBs = FB2[:, 3456:3968].rearrange("p (c n) -> p c n", c=2)
        WBs = FB2[:, 3968:4480].rearrange("p (c n) -> p c n", c=4)
        wgl = FB2[:, 4480:5056].rearrange("p (c n) -> p c n", c=16)
        selb = FB2[:, 5056:5184]
        gfin = FA[:, 8192:10240]
        M1s = FA[:, 10240:11264].rearrange("p (i j) -> p i j", i=16)
        Cs = FA[:, 11264:12288].rearrange("p (i j) -> p i j", i=16)
        gates = FA[:, 12288:12360].rearrange("p (s j) -> p s j", s=2)
        yacc = FA[:, 12400:12784]
        imp = FA[:, 12800:12864]
        sc = FA[:, 12864:12928]
        sc2 = FA[:, 12928:12992]
        m8 = FA[:, 12992:13008]
        zc = FA[:, 13008:13011]
        rcc = FA[:, 13012:13015]
        w0 = FA[:, 13016:13019]
        thr = FA[:, 13020:13021]
        rcw = FA[:, 13024:13027]
        w1_ = FA[:, 13028:13031]
        szq = szu
        qT = BA[:, 26624:29696].rearrange("p (s c n) -> p s c n", s=2, c=12)
        PS2 = Rot([(pA[0], "pA0"), (pA[1], "pA1")])
        PSE = Rot([(pse[i], "pse%d" % i) for i in range(3)])

        def l1_setup():
            P.op("sp", lambda e: e.dma_start(out=Es, in_=E_d), writes=["Es"], dma=True)
            P.op("pool", lambda e: e.memset(selT, 0.0), writes=["selT"])
            P.op("pool", lambda e: e.memset(selb, 0.0), writes=["selb"])
            P.op("sp", lambda e: e.dma_start(out=WBs, in_=WB_d), writes=["WBs"], dma=True)
            P.op("sp", lambda e: e.dma_start(out=wgl, in_=wb["w_gl"].rearrange("(c p) n -> p c n", p=128)), reads=["w_gl_b"], writes=["wgl"], dma=True)
            P.op("sp", lambda e: e.dma_start(out=gfin, in_=gfin_d.to_broadcast([128, D])), writes=["gfin"], dma=True)
            P.op("sp", lambda e: e.dma_start(out=M1s, in_=M1_d.rearrange("(i p) j -> p i j", p=128)), writes=["M1s"], dma=True)
            P.op("sp", lambda e: e.dma_start(out=Cs, in_=C_d.rearrange("(i p) j -> p i j", p=128)), writes=["Cs"], dma=True)
            for i in range(4):
                P.op("pool", lambda e, i=i: e.memset(vsg[i][:, :, 128:130], 0.0), writes=["vsg%d" % i])
                P.op("pool", lambda e, i=i: e.memset(vsg[i][:, :, 128:129], 1.0), reads=["vsg%d" % i], writes=["vsg%d" % i])
            for i in range(2):
                P.op("pool", lambda e, i=i: e.memset(vwg[i][:, :, 128:130], 0.0), writes=["vwg%d" % i])
                P.op("pool", lambda e, i=i: e.memset(vwg[i][:, :, 128:129], 1.0), reads=["vwg%d" % i], writes=["vwg%d" % i])

        def attention(i, s, g):
            qv = qT[:, s, 3 * g:3 * g + 3, :].rearrange("p a b -> p (a b)")
            as3 = lambda ap: ap
            gv = gates[:, s, 9 * g:9 * g + 9].rearrange("p (r k) -> p r k", r=3)
            for nc_ in range(2):
                pa, pres = PS2.next()
                mm(as3(pa[:, 0:384]), kcmpT[:, g, nc_ * 128:(nc_ + 1) * 128], qv, True, False, ["kcmpT", "qT"], [pres])
                mm(pa[:, 0:384].rearrange("p (a b) -> p a b", a=3), ident[:], bc(CBs[:, nc_, s * 128:(s + 1) * 128], 3), False, True, ["ident", "CBs"], [pres])
                evac("exp", pcm[nc_], pa[:, 0:384], [pres], ["pcm%d" % nc_], scale=SCALE)
            for r in range(3):
                for nc_ in range(2):
                    mm(pO[:, r * 256:r * 256 + 194], pcm[nc_][:, r * 128:(r + 1) * 128], vcmpx[:, nc_, g, :], nc_ == 0, nc_ == 1, ["pcm%d" % nc_, "vcmpx"], ["pO"])
            P.op("dve", lambda e: e.tensor_scalar(out=zc, in0=pO[:, 192:192 + 2 * 256 + 1:256], scalar1=1e-30, scalar2=None, op0=ALU.add), reads=["pO"], writes=["zc"])
            P.op("dve", lambda e: e.reciprocal(out=rcc, in_=zc), reads=["zc"], writes=["rcc"])
            P.op("dve", lambda e: e.tensor_scalar(out=imp, in0=pO[:, 128:192], scalar1=rcc[:, 0:1], scalar2=None, op0=ALU.mult), reads=["pO", "rcc"], writes=["imp"])
            for r in (1, 2):
                P.op("dve", lambda e, r=r: e.scalar_tensor_tensor(out=imp, in0=pO[:, r * 256 + 128:r * 256 + 192], scalar=rcc[:, r:r + 1], in1=imp, op0=ALU.mult, op1=ALU.add),
                     reads=["pO", "rcc", "imp"], writes=["imp"])
            P.op("dve", lambda e: e.tensor_tensor(out=w0, in0=rcc, in1=gv[:, :, 0], op=ALU.mult), reads=["rcc", "gates"], writes=["w0"])
            for r in range(3):
                P.op("dve", lambda e, r=r: e.tensor_scalar(out=yacc[:, r * 128:(r + 1) * 128], in0=pO[:, r * 256:r * 256 + 128], scalar1=w0[:, r:r + 1], scalar2=None, op0=ALU.mult),
                     reads=["pO", "w0"], writes=["yacc"])
            P.op("dve", lambda e: e.tensor_tensor(out=sc, in0=imp, in1=M1s[:, i, :], op=ALU.mult), reads=["imp", "M1s"], writes=["sc"])
            P.op("dve", lambda e: e.tensor_tensor(out=sc, in0=sc, in1=Cs[:, i, :], op=ALU.add), reads=["sc", "Cs"], writes=["sc"])
            P.op("dve", lambda e: e.max(out=m8[:, 0:8], in_=sc), reads=["sc"], writes=["m8"])
            P.op("dve", lambda e: e.match_replace(out=sc2, in_to_replace=m8[:, 0:8], in_values=sc, imm_value=-1e30), reads=["sc", "m8"], writes=["sc2"])
            P.op("dve", lambda e: e.max(out=m8[:, 8:16], in_=sc2), reads=["sc2"], writes=["m8"])
            P.op("dve", lambda e: e.tensor_scalar(out=thr, in0=m8[:, 15:16], scalar1=-1e29, scalar2=None, op0=ALU.max), reads=["m8"], writes=["thr"])
            P.op("dve", lambda e: e.tensor_scalar(out=sc2, in0=sc, scalar1=thr[:, 0:1], scalar2=None, op0=ALU.is_ge), reads=["sc", "thr"], writes=["sc2"])
            P.op("dve", lambda e: e.tensor_scalar(out=selb[:, 0:64], in0=sc2, scalar1=1.0, scalar2=-NEG, op0=ALU.subtract, op1=ALU.mult), reads=["sc2"], writes=["selb"])
            def wload(bw):
                P.op("sp", lambda e: e.dma_start(out=kwg[bw], in_=kwT_d[g, :, (12 + i) * 128:(17 + i) * 128]), reads=["kwT_d"], writes=["kwg%d" % bw], dma=True)
                P.op("sp", lambda e: e.dma_start(out=vwg[bw][:, :, 0:128], in_=vw_d[(12 + i) * 128:(17 + i) * 128, g * 128:(g + 1) * 128].rearrange("(c p) d -> p c d", p=128)),
                     reads=["vw_d"], writes=["vwg%d" % bw], dma=True)
            bw = KVW.next(wload)
            jobs = []
            for d in range(5):
                c = 12 + i + d
                if c < 16:
                    pat = 0 if d == 0 else 1
                else:
                    pat = 2 if d == 0 else (3 if d == 4 else None)
                mms = [(kwg[bw][:, d * 128:(d + 1) * 128], qv, ["kwg%d" % bw, "qT"])]
                if pat is not None:
                    mms.append((ident[:], bc(WBs[:, pat, :], 3), ["ident", "WBs"]))
                jobs.append(dict(mms=mms, v=vwg[bw][:, d, :], vres="vwg%d" % bw, acc=pA[3], accres="pA3", first=d == 0, last=d == 4, sel=False))
            nch = 17 + i
            ncg = (nch + 3) // 4

            def mk_loader(cg, n):
                def f(b):
                    P.op("sp", lambda e: e.dma_start(out=ksg[b][:, 0:n * 128], in_=ksT_d[g, :, cg * 512:cg * 512 + n * 128]), reads=["ksT_d"], writes=["ksg%d" % b], dma=True)
                    P.op("sp", lambda e: e.dma_start(out=vsg[b][:, 0:n, 0:128], in_=vs_d[cg * 512:cg * 512 + n * 128, g * 128:(g + 1) * 128].rearrange("(c p) d -> p c d", p=128)),
                         reads=["vs_d"], writes=["vsg%d" % b], dma=True)
                return f
            for cg in range(ncg):
                ncc = min(4, nch - cg * 4)
                jobs.append(dict(load=mk_loader(cg, ncc)))
                for cc in range(ncc):
                    c = cg * 4 + cc
                    last = (c == nch - 1)
                    jobs.append(dict(kc=(cc, c), last=last, first=c == 0, sel=True))

            cur_b = [None]

            def stageA(job):
                if job["sel"]:
                    b = cur_b[0]
                    cc, c = job["kc"]
                    mms = [(ksg[b][:, cc * 128:(cc + 1) * 128], qv, ["ksg%d" % b, "qT"]),
                           (Es[:, c * 128:(c + 1) * 128], selT, ["Es", "selT"])]
                    if job["last"]:
                        mms.append((ident[:], bc(WBs[:, 3, :], 3), ["ident", "WBs"]))
                    job.update(mms=mms, v=vsg[b][:, cc, :], vres="vsg%d" % b, acc=pA[2], accres="pA2")
                pa, pres = PS2.next()
                n = len(job["mms"])
                for k, (l, r_, rd) in enumerate(job["mms"]):
                    o_ = pa[:, 0:384].rearrange("p (a b) -> p a b", a=3) if len(r_.shape) == 3 else pa[:, 0:384]
                    mm(o_, l, r_, k == 0, k == n - 1, rd, [pres])
                pe_, peres = PSE.next()
                evac("exp", pe_, pa[:, 0:384], [pres], [peres], scale=SCALE)
                job["pe"] = (pe_, peres)

            def stageB(job):
                pe_, peres = job["pe"]
                for r in range(3):
                    mm(job["acc"][:, r * 130:(r + 1) * 130], pe_[:, r * 128:(r + 1) * 128], job["v"], job["first"], job["last"], [peres, job["vres"]], [job["accres"]])

            def win_epilogue():
                P.op("dve", lambda e: e.reciprocal(out=rcw, in_=pA[3][:, 128:128 + 2 * 130 + 1:130]), reads=["pA3"], writes=["rcw"])
                P.op("dve", lambda e: e.tensor_tensor(out=w1_, in0=rcw, in1=gv[:, :, 2], op=ALU.mult), reads=["rcw", "gates"], writes=["w1_"])
                for r in range(3):
                    P.op("dve", lambda e, r=r: e.scalar_tensor_tensor(out=yacc[:, r * 128:(r + 1) * 128], in0=pA[3][:, r * 130:r * 130 + 128], scalar=w1_[:, r:r + 1], in1=yacc[:, r * 128:(r + 1) * 128], op0=ALU.mult, op1=ALU.add),
                         reads=["pA3", "w1_", "yacc"], writes=["yacc"])

            prev = None
            transposed = False
            for job in jobs:
                if "load" in job:
                    cur_b[0] = KVS.next(job["load"])
                    continue
                if job["sel"] and not transposed:
                    bank, bres = PT.next()
                    P.op("pe", lambda e, bank=bank: e.transpose(out=bank[:, 0:128], in_=selb, identity=ident[:]), reads=["selb", "ident"], writes=[bres])
                    P.op("act", lambda e, bank=bank: e.copy(out=selT[0:64, :].rearrange("p (a b) -> p a b", a=3), in_=bc(bank[0:64, 0:128], 3)), [bres], ["selT"])
                    transposed = True
                stageA(job)
                if prev is not None:
                    stageB(prev)
                    if (not prev["sel"]) and prev["last"]:
                        win_epilogue()
                prev = job
            stageB(prev)
            P.op("dve", lambda e: e.reciprocal(out=rcc, in_=pA[2][:, 128:128 + 2 * 130 + 1:130]), reads=["pA2"], writes=["rcc"])
            P.op("dve", lambda e: e.tensor_tensor(out=w0, in0=rcc, in1=gv[:, :, 1], op=ALU.mult), reads=["rcc", "gates"], writes=["w0"])
            for r in range(3):
                h = 3 * g + r
                P.op("dve", lambda e, r=r, h=h: e.scalar_tensor_tensor(out=ybf[:, h * 128:(h + 1) * 128], in0=pA[2][:, r * 130:r * 130 + 128], scalar=w0[:, r:r + 1], in1=yacc[:, r * 128:(r + 1) * 128], op0=ALU.mult, op1=ALU.add),
                     reads=["pA2", "w0", "yacc"], writes=["ybf"])

        def layer1_tile(tt):
            r0 = tt * 256
            T0 = HALF + r0
            for s in range(2):
                P.op("sp", lambda e, s=s: e.dma_start(out=xt[:, s, :], in_=h1_d[r0 + s * 128:r0 + (s + 1) * 128, :]), reads=["h1_d"], writes=["xt%d" % s], dma=True)
            P.op("sp", lambda e: e.dma_start(out=cst[0:32, :, 0:256], in_=cs_d[:, :, T0:T0 + 256].rearrange("k p n -> p k n")), reads=["cs_d"], writes=["cst"], dma=True)
            P.op("sp", lambda e: e.dma_start(out=CBs, in_=CB_d[:, :, r0:r0 + 256].rearrange("c p n -> p c n")), writes=["CBs"], dma=True)
            rmsnorm_T(1, 2, None)

            def sink_q(cb):
                def f(j, pa, pres):
                    h = cb * 4 + j
                    st, stres = KST.next()
                    evac("copy", st[:, 0:256], pa[:, 0:256], [pres], [stres])
                    pr, prres = PA.next()
                    mm(pr[:, 0:256], rperm[:, :], st[:, 0:256], True, True, ["rperm", stres], [prres])
                    r1 = FA[0:32, 18720:18720 + 256]
                    r2 = FA[0:32, 17696:17696 + 256]
                    P.op("dve", lambda e, st=st, r1=r1: e.tensor_tensor(out=r1, in0=st[0:32, 0:256], in1=cst[0:32, 0, 0:256], op=ALU.mult), reads=[stres, "cst"], writes=["r1"])
                    P.op("dve", lambda e, pr=pr, r2=r2: e.tensor_tensor(out=r2, in0=pr[0:32, 0:256], in1=cst[0:32, 1, 0:256], op=ALU.mult), reads=[prres, "cst"], writes=["rzb"])
                    P.op("dve", lambda e, st=st, r1=r1, r2=r2: e.tensor_tensor(out=st[0:32, 0:256], in0=r1, in1=r2, op=ALU.add), reads=["r1", "rzb"], writes=[stres])
                    P.op("pool", lambda e, st=st, h=h: e.tensor_copy(out=qT[:, :, h, :], in_=st[:, 0:256].rearrange("p (s n) -> p s n", s=2)), reads=[stres], writes=["qT"])
                return f

            def sink_zq(cb):
                def f(j, pa, pres):
                    evac("silu", szq[:, (cb - 3) * 4 + j, 0:256], pa[:, 0:256], [pres], ["szu"])
                return f

            def sink_qm(j, pa, pres):
                evac("copy", qmT[:, j, 0:256], pa[:, 0:256], [pres], ["qmT"], eng="dve")

            def sink_zm(j, pa, pres):
                evac("silu", szm[:, j, 0:256], pa[:, 0:256], [pres], ["szm"])

            for cb in (0, 1, 2):
                proj_fm("w_in1", cb, 256, sink_q(cb))
            for cb in (3, 4, 5):
                proj_fm("w_in1", cb, 256, sink_zq(cb))
            proj_fm("w_in1", 6, 256, sink_qm)
            proj_fm("w_in1", 7, 256, sink_zm)
            for s in range(2):
                pa, pres = PA.next()
                for kc in range(16):
                    mm(pa[:, 0:36], hT[:, kc, s * 128:(s + 1) * 128], wgl[:, kc, :], kc == 0, kc == 15, ["hT", "wgl"], [pres])
                P.op("act", lambda e, s=s, pa=pa: e.activation(out=gates[:, s, :], in_=pa[:, 0:36], func=AF.Sigmoid), reads=[pres], writes=["gates"])
            mem_attn(256, 12)
            for s in range(2):
                i = 2 * tt + s
                for g in range(4):
                    attention(i, s, g)
                for c in range(12):
                    bank, bres = PT.next()
                    P.op("pe", lambda e, c=c, bank=bank: e.transpose(out=bank[:, 0:128], in_=ybf[:, c * 128:(c + 1) * 128], identity=ident[:]), reads=["ybf", "ident"], writes=[bres])
                    P.op("dve", lambda e, c=c, s=s, bank=bank: e.tensor_tensor(out=yT[:, c, s * 128:(s + 1) * 128], in0=bank[:, 0:128], in1=szq[:, c, s * 128:(s + 1) * 128], op=ALU.mult),
                         reads=[bres, "szu"], writes=["HY"])
            out_proj("w_out1", 2)
            for s in range(2):
                P.op("act", lambda e, s=s: e.activation(out=hb[:, s, :], in_=xt[:, s, :], func=AF.Square, accum_out=ss[:, s:s + 1]),
                     reads=["xt%d" % s], writes=["HY", "ss%d" % s])
                P.op("act", lambda e, s=s: e.activation(out=rs[:, s:s + 1], in_=ss[:, s:s + 1], func=AF.Sqrt, scale=1.0 / D, bias=epsb[:]),
                     reads=["ss%d" % s, "epsb"], writes=["rs%d" % s])
                P.op("dve", lambda e, s=s: e.reciprocal(out=rs[:, s:s + 1], in_=rs[:, s:s + 1]), reads=["rs%d" % s], writes=["rs%d" % s])
                P.op("dve", lambda e, s=s: e.scalar_tensor_tensor(out=xt[:, s, :], in0=xt[:, s, :], scalar=rs[:, s:s + 1], in1=gfin, op0=ALU.mult, op1=ALU.mult),
                     reads=["xt%d" % s, "rs%d" % s, "gfin"], writes=["xt%d" % s])
                P.op("sp", lambda e, s=s: e.dma_start(out=out_d[r0 + s * 128:r0 + (s + 1) * 128, :], in_=xt[:, s, :]), reads=["xt%d" % s], writes=["out_d"], dma=True)

        def program():
            setup()
            P.barrier(skip=("pool",))
            mem_kv(0)
            for tt in range(NT0):
                layer0_tile(tt)
            P.barrier()
            if NT0 == 8 and "nol1" not in dbg:
                compress()
                P.barrier()
                l1_setup()
                mem_kv(1)
                for tt in range(NT1):
                    layer1_tile(tt)
            if "kv" in dbg:
                for nm, src in (("ksT", ksT_d), ("kcT", kcT_d), ("kwT", kwT_d)):
                    for g in range(4):
                        P.op("sp", lambda e, nm=nm, src=src, g=g: e.dma_start(out=dbg_out[nm][g], in_=src[g]), writes=["dbg_" + nm], dma=True)
                P.op("sp", lambda e: e.dma_start(out=dbg_out["vs"], in_=vs_d), writes=["dbg_vs"], dma=True)

        NT0 = int(os.environ.get("K_NT0", "8"))
        NT1 = int(os.environ.get("K_NT1", "8"))
        P.dry = True
        program()
        P.dry = False
        W.n = 0
        PA.i = PT.i = KST.i = PS2.i = PSE.i = 0
        KVS.n = KVW.n = 0
        program()
        P.emit(nc, es)
        if os.environ.get('K_STATS'):
            print('ops', {e: sum(1 for o in P.ops if o.eng == e) for e in ENGS}, 'waits', sum(len(o.waits) for o in P.ops), 'sigs', sum(1 for o in P.ops if o.sig))
    return nc


def _bf(a):
    return np.asarray(a, dtype=np.float32).astype(ml_dtypes.bfloat16)


def host_consts(p):
    c = {}
    c["ident"] = _bf(np.eye(128))
    rp = np.zeros((128, 128), np.float32)
    for i in range(32):
        rp[i + 16 if i < 16 else i - 16, i] = 1.0
    c["rperm"] = _bf(rp)
    cp = np.zeros((128, 2, 4, 16), np.float32)
    t1 = np.arange(1, 17, dtype=np.float32)
    for g, w in enumerate(WINS):
        seq = 1.0 / np.minimum(t1, float(w))
        cp[:, 0, g, :] = seq
        cp[:, 1, g, :] = seq if p == 0 else 1.0 / w
    c["cpool"] = cp
    c["hflag"] = np.full((128, 1), float(p), np.float32)
    half = 16
    inv = (500000.0 ** (-np.arange(half, dtype=np.float32) * 2.0 / 32)).astype(np.float32)
    rc = np.zeros((32, 4), np.float32)
    rc[:, 0] = np.concatenate([inv, inv])
    rc[:, 1] = 0.5 * np.pi
    rc[:16, 2] = np.pi
    rc[16:, 2] = 0.0
    c["ropec"] = rc
    own0 = 2048 * p
    tq = own0 + np.arange(HALF)
    sstart = np.where(np.arange(64) < 32, (1 - p) * 2048, p * 2048) + 64 * (np.arange(64) % 32)
    cur64 = (tq // 64) * 64
    ok = sstart[None, :] <= tq[:, None]
    forced = (sstart[None, :] == 0) | (sstart[None, :] == cur64[:, None]) | (sstart[None, :] == cur64[:, None] - 64)
    forced &= ok
    c["M1"] = (ok & ~forced).astype(np.float32)
    c["Ctab"] = np.where(ok, np.where(forced, 1e4, 0.0), -1e30).astype(np.float32)
    n = np.arange(256)
    reg_start = np.where(n < 128, (1 - p) * 2048, p * 2048)
    cend = reg_start + 16 * (n % 128) + 31
    cend[127] = 1 << 30 if p == 0 else cend[127]
    cend[255] = 1 << 30
    cb = np.where(cend[:, None] <= tq[None, :], 0.0, NEG)
    c["CB"] = _bf(cb.reshape(2, 128, HALF))
    E = np.zeros((128, S), np.float32)
    E[np.arange(S) // 64, np.arange(S)] = 1.0
    c["Eall"] = _bf(E)
    m = np.arange(128)[:, None]
    q = np.arange(128)[None, :]
    WBt = np.zeros((128, 4, 128), np.float32)
    other_ok = (p == 1)
    WBt[:, 0, :] = np.where((m > q) & other_ok, 0.0, NEG)
    WBt[:, 1, :] = 0.0 if other_ok else NEG
    WBt[:, 2, :] = np.where(m > q, 0.0, NEG)
    WBt[:, 3, :] = np.where(m <= q, 0.0, NEG)
    c["WB"] = _bf(WBt)
    cs_, ce_ = 16 * n, 16 * n + 31
    ss_ = 64 * np.arange(64)
    ov = np.minimum(ce_[:, None], ss_[None, :] + 63) - np.maximum(cs_[:, None], ss_[None, :]) + 1
    c["ovl"] = _bf((np.maximum(ov, 0) / 32.0).reshape(2, 128, 64))
    return c


def host_inputs(inputs):
    x = np.asarray(inputs["x"], np.float32)
    mem = np.asarray(inputs["mem"], np.float32)
    pos = np.asarray(inputs["positions"], np.int32)
    shared = {}
    shared["w_in0"] = np.ascontiguousarray(inputs["a_w_in"][0], np.float32)
    bw = np.asarray(inputs["b_w_in"][0], np.float32)
    shared["w_in1"] = np.ascontiguousarray(np.concatenate([bw[:, 0:1536], bw[:, 1572:3108], bw[:, 3108:3620], bw[:, 3620:4132]], axis=1))
    shared["w_gl"] = np.ascontiguousarray(bw[:, 1536:1572])
    shared["w_out0"] = np.ascontiguousarray(inputs["w_out"][0], np.float32)
    shared["w_out1"] = np.ascontiguousarray(inputs["w_out"][1], np.float32)
    shared["w_kv"] = np.ascontiguousarray(inputs["w_kv"], np.float32)
    shared["w_mkv0"] = np.ascontiguousarray(inputs["w_mem_kv"][0], np.float32)
    shared["w_mkv1"] = np.ascontiguousarray(inputs["w_mem_kv"][1], np.float32)
    shared["w_pool"] = np.ascontiguousarray(np.asarray(inputs["a_w_pool"][0], np.float32).reshape(1536, 384))
    shared["cw1"] = np.ascontiguousarray(np.asarray(inputs["cmp_w1"], np.float32).reshape(8192, 256))
    shared["cw2"] = np.ascontiguousarray(np.asarray(inputs["cmp_w2"], np.float32).reshape(512, 128))
    fm = lambda v: np.asarray(v, np.float32).reshape(16, 128).T
    ng, mg = np.asarray(inputs["norm_g"]), np.asarray(inputs["mem_norm_g"])
    shared["gT"] = np.ascontiguousarray(np.stack([fm(ng[0]), fm(ng[1]), fm(inputs["kv_norm_g"]), fm(mg[0]), fm(mg[1])], axis=1))
    shared["gfin"] = np.asarray(inputs["final_g"], np.float32).reshape(1, D)
    shared["pscale"] = np.ascontiguousarray(np.asarray(inputs["a_pool_scale"][0], np.float32).reshape(12, 128).T)
    shared["peT"] = np.ascontiguousarray(np.asarray(inputs["cmp_pe"], np.float32).transpose(2, 0, 1))
    consts = [host_consts(0), host_consts(1)]
    maps = []
    for b in range(4):
        for p in range(2):
            own = slice(2048 * p, 2048 * p + 2048)
            oth = slice(2048 * (1 - p), 2048 * (1 - p) + 2048)
            m = dict(shared)
            m.update(consts[p])
            m["x"] = np.ascontiguousarray(np.concatenate([x[b, oth], x[b, own]], axis=0))
            m["pos"] = np.ascontiguousarray(np.concatenate([pos[b, oth], pos[b, own]])[None, :])
            m["mem"] = np.ascontiguousarray(mem[b])
            maps.append(m)
    return maps


_NC_CACHE = {}


def kernel(**inputs):
    dbg = tuple(x for x in os.environ.get("K_DBG", "").split(",") if x)
    maps = host_inputs(inputs)
    if dbg not in _NC_CACHE:
        _NC_CACHE[dbg] = build(dbg)
    nc = _NC_CACHE[dbg]
    ncores = int(os.environ.get("K_NCORES", "8"))
    res = run_bass_kernel_spmd(nc, maps[:ncores], core_ids=list(range(ncores)))
    if dbg:
        return res
    out = np.zeros((4, S, D), np.float32)
    for b in range(4):
        for p in range(2):
            out[b, 2048 * p:2048 * p + 2048] = res.results[2 * b + p]["out"]
    return out
```

```python
import os
import numpy as np
import ml_dtypes
from contextlib import ExitStack
import concourse.bass as bass
import concourse.mybir as mybir
from concourse.bass_utils import run_bass_kernel_spmd

F32 = mybir.dt.float32
BF16 = mybir.dt.bfloat16
I32 = mybir.dt.int32
ALU = mybir.AluOpType
AF = mybir.ActivationFunctionType

ENGS = ("pe", "act", "dve", "pool", "sp")
NDMA_SEMS = 12


class Op:
    __slots__ = ("eng", "fn", "reads", "writes", "dma", "waits", "sig", "idx")

    def __init__(self, eng, fn, reads, writes, dma):
        self.eng, self.fn, self.reads, self.writes, self.dma = eng, fn, reads, writes, dma
        self.waits = []
        self.sig = None


class Prog:
    def __init__(self):
        self.ops = []
        self.dry = False

    def op(self, eng, fn, reads=(), writes=(), dma=False):
        if self.dry:
            return None
        o = Op(eng, fn, tuple(reads), tuple(writes), dma)
        self.ops.append(o)
        return o

    def barrier(self, skip=()):
        if not self.dry:
            self.ops.append(("bar", tuple(skip)))

    def analyze(self):
        last_w = {}
        readers = {}
        cnt = {e: 0 for e in ENGS}
        dma_n = {e: 0 for e in ENGS}
        dma_hist = {e: [] for e in ENGS}
        sigval = {}
        deps_of = []
        ops2 = []
        pend = {e: set() for e in ENGS}
        last_c = {}
        for o in self.ops:
            if isinstance(o, tuple):
                continue
            o.idx = len(ops2)
            ops2.append(o)
        ops_all = self.ops
        self.ops = ops2
        for o in ops_all:
            if isinstance(o, tuple):
                bd = set(last_c.values())
                for q in ENGS:
                    if q in o[1]:
                        continue
                    bd |= set(dma_hist[q][-NDMA_SEMS:])
                for e in ENGS:
                    pend[e] |= bd
                continue
            deps = set(pend[o.eng])
            pend[o.eng] = set()
            if not o.dma:
                last_c[o.eng] = o.idx
            for r in o.reads:
                deps.update(last_w.get(r, ()))
            for r in o.writes:
                deps.update(last_w.get(r, ()))
                for rd in readers.get(r, ()):
                    deps.add(rd)
            deps.discard(o.idx)
            keep = set()
            for d in deps:
                p = self.ops[d]
                if p.eng == o.eng and not p.dma and not o.dma:
                    if o.eng == "pe":
                        continue
                    raw = any((r in p.writes) for r in o.reads)
                    if not raw and o.eng != "pool":
                        continue
                keep.add(d)
            if o.dma:
                h = dma_hist[o.eng]
                if len(h) >= NDMA_SEMS:
                    keep.add(h[-NDMA_SEMS])
                h.append(o.idx)
            deps_of.append(keep)
            for r in o.writes:
                if o.dma:
                    last_w[r] = [w for w in last_w.get(r, ()) if self.ops[w].dma] + [o.idx]
                else:
                    last_w[r] = [o.idx]
                readers[r] = []
            for r in o.reads:
                if r not in o.writes:
                    lst = readers.setdefault(r, [])
                    if not o.dma:
                        lst[:] = [x for x in lst if self.ops[x].dma or self.ops[x].eng != o.eng]
                    lst.append(o.idx)
        needed = set()
        for k in deps_of:
            needed |= k
        for o in self.ops:
            if o.dma:
                m = dma_n[o.eng]
                dma_n[o.eng] += 1
                key = ("dma", o.eng, m % NDMA_SEMS)
                o.sig = (key, 16)
                sigval[o.idx] = (key, 16 * (m // NDMA_SEMS + 1))
            elif o.idx in needed:
                cnt[o.eng] += 1
                key = ("eng", o.eng)
                o.sig = (key, 1)
                sigval[o.idx] = (key, cnt[o.eng])
        waited = {e: {} for e in ENGS}
        for o in self.ops:
            best = {}
            for d in deps_of[o.idx]:
                key, v = sigval[d]
                if v > best.get(key, 0):
                    best[key] = v
            for key, v in best.items():
                if waited[o.eng].get(key, 0) >= v:
                    continue
                waited[o.eng][key] = v
                o.waits.append((key, v))
        self.final = {}
        for o in self.ops:
            if o.dma:
                key, v = sigval[o.idx]
                self.final[key] = max(self.final.get(key, 0), v)

    def emit(self, nc, es):
        self.analyze()
        sems = {}

        def sem(key):
            if key not in sems:
                sems[key] = es.enter_context(nc.semaphore("s_" + "_".join(str(k) for k in key)))
            return sems[key]

        for o in self.ops:
            if o.sig:
                sem(o.sig[0])
        per = {e: [o for o in self.ops if o.eng == e] for e in ENGS}
        block = es.enter_context(nc.Block())

        def run(engobj, name):
            for o in per[name]:
                for key, v in o.waits:
                    engobj.wait_ge(sem(key), v)
                ins = o.fn(engobj)
                if o.sig:
                    ins.then_inc(sem(o.sig[0]), o.sig[1])
            if name == "sp":
                for key, v in self.final.items():
                    engobj.wait_ge(sem(key), v)

        @block.tensor
        def _(e):
            run(e, "pe")

        @block.scalar
        def _(e):
            run(e, "act")

        @block.vector
        def _(e):
            run(e, "dve")

        @block.gpsimd
        def _(e):
            run(e, "pool")

        @block.sync
        def _(e):
            run(e, "sp")


D = 2048
S = 4096
HALF = 2048
MEM = 256
EPS = 1e-6
SCALE = 128.0 ** -0.5
NEG = -30000.0
WINS = (2, 4, 8, 16)
TWO_PI = 2.0 * np.pi


def bc(ap2d, reps):
    return bass.AP(ap2d.tensor, ap2d.offset, [list(ap2d.ap[0]), [0, reps], list(ap2d.ap[1])])


def build(dbg=()):
    nc = bass.Bass("TRN2", target_bir_lowering=False)
    P = Prog()

    def din(name, shape, dt=F32):
        return nc.dram_tensor(name, list(shape), dt, kind="ExternalInput").ap()

    def dscr(name, shape, dt):
        return nc.dram_tensor(name, list(shape), dt, kind="Internal").ap()

    x_d = din("x", [S, D])
    pos_d = din("pos", [1, S], I32)
    mem_d = din("mem", [MEM, D])
    wf = {
        "w_in0": din("w_in0", [D, 4096]), "w_in1": din("w_in1", [D, 4096]),
        "w_out0": din("w_out0", [D, D]), "w_out1": din("w_out1", [D, D]),
        "w_kv": din("w_kv", [D, 3072]), "w_gl": din("w_gl", [D, 36]),
        "w_mkv0": din("w_mkv0", [D, 1024]), "w_mkv1": din("w_mkv1", [D, 1024]),
        "w_pool": din("w_pool", [1536, 384]), "cw1": din("cw1", [8192, 256]),
        "cw2": din("cw2", [512, 128]),
    }
    wb = {k: dscr(k + "_b", v.shape, BF16) for k, v in wf.items()}
    gT_d = din("gT", [128, 5, 16])
    gfin_d = din("gfin", [1, D])
    pscale_d = din("pscale", [128, 12])
    peT_d = din("peT", [128, 2, 32])
    cpool_d = din("cpool", [128, 2, 4, 16])
    hflag_d = din("hflag", [128, 1])
    rope_d = din("ropec", [32, 4])
    ident_d = din("ident", [128, 128], BF16)
    rperm_d = din("rperm", [128, 128], BF16)
    M1_d = din("M1", [HALF, 64])
    C_d = din("Ctab", [HALF, 64])
    CB_d = din("CB", [2, 128, HALF], BF16)
    E_d = din("Eall", [128, S], BF16)
    WB_d = din("WB", [128, 4, 128], BF16)
    ovl_d = din("ovl", [2, 128, 64], BF16)

    h1_d = dscr("h1_s", [HALF, D], F32)
    kcT_d = dscr("kcT_s", [4, 128, S], BF16)
    vcT_d = dscr("vcT_s", [4, 128, S], BF16)
    ksT_d = dscr("ksT_s", [4, 128, S], BF16)
    kwT_d = dscr("kwT_s", [4, 128, S], BF16)
    vs_d = dscr("vs_s", [S, 512], BF16)
    vw_d = dscr("vw_s", [S, 512], BF16)
    cs_d = dscr("cs_s", [2, 32, S], BF16)

    DN = {id(kcT_d): "kcT_d", id(vcT_d): "vcT_d", id(ksT_d): "ksT_d", id(kwT_d): "kwT_d", id(vs_d): "vs_d", id(vw_d): "vw_d"}
    out_d = nc.dram_tensor("out", [HALF, D], F32, kind="ExternalOutput").ap()
    dbg_out = {}
    if "h1" in dbg:
        dbg_out["h1"] = nc.dram_tensor("dbg_h1", [HALF, D], F32, kind="ExternalOutput").ap()
    if "kv" in dbg:
        dbg_out["ksT"] = nc.dram_tensor("dbg_ksT", [4, 128, S], BF16, kind="ExternalOutput").ap()
        dbg_out["kcT"] = nc.dram_tensor("dbg_kcT", [4, 128, S], BF16, kind="ExternalOutput").ap()
        dbg_out["vs"] = nc.dram_tensor("dbg_vs", [S, 512], BF16, kind="ExternalOutput").ap()
        dbg_out["kwT"] = nc.dram_tensor("dbg_kwT", [4, 128, S], BF16, kind="ExternalOutput").ap()

    with ExitStack() as es:
        def sb(n, s, d):
            return es.enter_context(nc.sbuf_tensor("sb_" + n, list(s), d))

        def ps(n, s, d):
            return es.enter_context(nc.psum_tensor("ps_" + n, list(s), d))

        ident = sb("ident", [128, 128], BF16)
        ones = sb("ones", [128, 128], BF16)
        rperm = sb("rperm", [128, 128], BF16)
        gT = sb("gT", [128, 5, 16], F32)
        pscale = sb("pscale", [128, 12], F32)
        cpool = sb("cpool", [128, 2, 4, 16], F32)
        hflag = sb("hflag", [128, 1], F32)
        ropec = sb("ropec", [32, 4], F32)
        epsb = sb("epsb", [128, 1], F32)
        ss = sb("ss", [128, 8], F32)
        rs = sb("rs", [128, 8], F32)
        wbuf = [sb("wbuf%d" % i, [128, 16, 512], BF16) for i in range(2)]
        mkT = sb("mkT", [128, 4, 256], BF16)
        mv = sb("mv", [128, 2, 512], BF16)
        kcmpT = sb("kcmpT", [128, 4, 256], BF16)
        vcmpx = sb("vcmpx", [128, 2, 4, 194], BF16)
        FA = sb("FA", [128, 19456], F32)
        BA = sb("BA", [128, 36864], BF16)
        pA = [ps("pA%d" % i, [128, 512], F32) for i in range(4)]
        pT = [ps("pT%d" % i, [128, 1024], BF16) for i in range(2)]
        pO = ps("pO", [128, 1024], F32)

        class Rot:
            def __init__(self, items):
                self.items, self.i = items, 0

            def next(self):
                it = self.items[self.i % len(self.items)]
                self.i += 1
                return it

        PA = Rot([(pA[i], "pA%d" % i) for i in range(4)])
        PT = Rot([(pT[i], "pT%d" % i) for i in range(2)])

        class WS:
            def __init__(self):
                self.sched, self.n, self.loaded = [], 0, 0

            def _load(self, k):
                name, cb = self.sched[k]
                slot = wbuf[k % 2]
                src = wb[name][:, cb * 512:(cb + 1) * 512].rearrange("(kc p) n -> p kc n", p=128)
                P.op("sp", lambda e: e.dma_start(out=slot[:], in_=src), reads=["%s_b%d" % (name, cb)],
                     writes=["wbuf%d" % (k % 2)], dma=True)

            def next(self, name, cb):
                k = self.n
                self.n += 1
                if P.dry:
                    self.sched.append((name, cb))
                    return wbuf[k % 2], "wbuf%d" % (k % 2)
                assert self.sched[k] == (name, cb)
                while self.loaded <= min(k + 1, len(self.sched) - 1):
                    self._load(self.loaded)
                    self.loaded += 1
                return wbuf[k % 2], "wbuf%d" % (k % 2)

        W = WS()

        class Stream:
            def __init__(self, nbuf, look):
                self.sched, self.n, self.loaded, self.nbuf, self.look = [], 0, 0, nbuf, look

            def next(self, loader):
                k = self.n
                self.n += 1
                if P.dry:
                    self.sched.append(loader)
                    return k % self.nbuf
                while self.loaded <= min(k + self.look, len(self.sched) - 1):
                    self.sched[self.loaded](self.loaded % self.nbuf)
                    self.loaded += 1
                return k % self.nbuf

        KVS = Stream(4, 2)
        KVW = Stream(2, 1)

        def evac(kind, out, in_, reads, writes, eng=None, scale=None):
            if kind == "copy":
                if eng == "dve":
                    P.op("dve", lambda e: e.tensor_copy(out=out, in_=in_), reads, writes)
                else:
                    P.op("act", lambda e: e.copy(out=out, in_=in_), reads, writes)
            elif kind == "silu":
                P.op("act", lambda e: e.activation(out=out, in_=in_, func=AF.Silu), reads, writes)
            elif kind == "exp":
                P.op("act", lambda e: e.activation(out=out, in_=in_, func=AF.Exp, scale=scale), reads, writes)

        def mm(out, lhsT, rhs, start, stop, reads, writes):
            P.op("pe", lambda e: e.matmul(out, lhsT=lhsT, rhs=rhs, start=start, stop=stop), reads, writes)

        xt = FA[:, 0:8192].rearrange("p (s n) -> p s n", s=4)
        uT = FA[:, 8192:8192 + 12 * 528].rearrange("p (c n) -> p c n", c=12)
        o = 8192 + 12 * 528
        ta = FA[:, o:o + 1584].rearrange("p (c n) -> p c n", c=3)
        tb = FA[:, o + 1584:o + 3168].rearrange("p (c n) -> p c n", c=3)
        o += 3168
        rzb = FA[:, o:o + 512]
        tm2 = FA[:, o + 512:o + 1024]
        o += 1024
        rtmp = FA[:, o:o + 256]
        hb = BA[:, 0:8192].rearrange("p (s n) -> p s n", s=4)
        yT = BA[:, 0:8192].rearrange("p (c n) -> p c n", c=16)
        hT = BA[:, 8192:16384].rearrange("p (c n) -> p c n", c=16)
        szu = BA[:, 16384:16384 + 6144].rearrange("p (c n) -> p c n", c=12)
        qmT = BA[:, 22528:22528 + 2048].rearrange("p (c n) -> p c n", c=4)
        szm = BA[:, 24576:24576 + 2048].rearrange("p (c n) -> p c n", c=4)
        poolb = [BA[:, 26624 + i * 1536:26624 + (i + 1) * 1536].rearrange("p (c n) -> p c n", c=3) for i in range(2)]
        pexp = BA[:, 29696:29696 + 1024].rearrange("p (c n) -> p c n", c=2)
        kst = [BA[:, 30720 + i * 512:30720 + (i + 1) * 512] for i in range(3)]
        cst = BA[:, 32256:32256 + 1024].rearrange("p (c n) -> p c n", c=2)
        wpool = sb("wpool", [128, 12, 384], BF16)
        posi_t = sb("posi", [32, 1024], I32)
        posn_t = [sb("posn0", [32, 1024], I32)] * 2
        KST = Rot([(kst[i], "kst%d" % i) for i in range(3)])

        def setup():
            order = [("w_mkv0", (0, 1)), ("w_in0", (3, 4, 5, 0, 1, 2, 6, 7)), ("w_pool", None), ("w_out0", (0, 1, 2, 3)),
                     ("w_kv", (0, 1, 2, 4, 3, 5)), ("cw1", None), ("cw2", None), ("w_mkv1", (0, 1)),
                     ("w_in1", (0, 1, 2, 3, 4, 5, 6, 7)), ("w_gl", None), ("w_out1", (0, 1, 2, 3))]
            for k, cbs in order:
                src, dst = wf[k], wb[k]
                if cbs is None:
                    P.op("pool", lambda e, s=src, d=dst: e.dma_start(out=d, in_=s), reads=[], writes=[k + "_b"], dma=True)
                else:
                    for cb in cbs:
                        P.op("pool", lambda e, s=src[:, cb * 512:(cb + 1) * 512], d=dst[:, cb * 512:(cb + 1) * 512]: e.dma_start(out=d, in_=s),
                             reads=[], writes=["%s_b%d" % (k, cb)], dma=True)
            for t, d, nm in ((ident, ident_d, "ident"), (rperm, rperm_d, "rperm"), (gT, gT_d, "gT"),
                             (pscale, pscale_d, "pscale"), (cpool, cpool_d, "cpool"), (hflag, hflag_d, "hflag"),
                             (ropec, rope_d, "ropec")):
                P.op("sp", lambda e, t=t, d=d: e.dma_start(out=t[:], in_=d), writes=[nm], dma=True)
            P.op("sp", lambda e: e.dma_start(out=wpool[:], in_=wb["w_pool"].rearrange("(g p) n -> p g n", p=128)),
                 reads=["w_pool_b"], writes=["wpool"], dma=True)
            P.op("pool", lambda e: e.memset(ones[:], 1.0), writes=["ones"])
            P.op("pool", lambda e: e.memset(epsb[:], EPS), writes=["epsb"])
            for pc in range(4):
                pi = posi_t
                pf = FA[0:32, 1024 * pc:1024 * (pc + 1)]
                P.op("sp", lambda e, pi=pi, pc=pc: e.dma_start(out=pi[:], in_=pos_d[:, pc * 1024:(pc + 1) * 1024].to_broadcast([32, 1024])),
                     writes=["posi"], dma=True)
                P.op("dve", lambda e, pi=pi, pf=pf: e.tensor_copy(out=pf, in_=pi[:]), reads=["posi"], writes=["pf%d" % pc])
                for kind in range(2):
                    ang = FA[0:32, 4096 + 1024 * kind:4096 + 1024 * (kind + 1)]
                    nf = FA[0:32, 6144 + 1024 * kind:6144 + 1024 * (kind + 1)]
                    ni = posn_t[kind]
                    tab = BA[0:32, 1024 * kind:1024 * (kind + 1)]
                    A, N_, NI = "ang%d" % kind, "nf%d" % kind, "ni"
                    P.op("dve", lambda e, ang=ang, pf=pf, kind=kind: e.tensor_scalar(
                        out=ang, in0=pf, scalar1=ropec[:, 0:1], scalar2=ropec[:, 1 + kind:2 + kind], op0=ALU.mult, op1=ALU.add),
                        reads=["pf%d" % pc, "ropec"], writes=[A])
                    P.op("dve", lambda e, ang=ang, ni=ni: e.tensor_scalar(
                        out=ni[:], in0=ang, scalar1=1.0 / TWO_PI, scalar2=None, op0=ALU.mult), reads=[A], writes=[NI])
                    P.op("dve", lambda e, nf=nf, ni=ni: e.tensor_copy(out=nf, in_=ni[:]), reads=[NI], writes=[N_])
                    P.op("dve", lambda e, ang=ang, nf=nf: e.scalar_tensor_tensor(
                        out=ang, in0=nf, scalar=-TWO_PI, in1=ang, op0=ALU.mult, op1=ALU.add), reads=[N_, A], writes=[A])
                    P.op("dve", lambda e, ang=ang, nf=nf: e.tensor_scalar(
                        out=nf, in0=ang, scalar1=float(np.pi), scalar2=-TWO_PI, op0=ALU.is_gt, op1=ALU.mult), reads=[A], writes=[N_])
                    P.op("dve", lambda e, ang=ang, nf=nf: e.tensor_tensor(out=ang, in0=ang, in1=nf, op=ALU.add), reads=[A, N_], writes=[A])
                    P.op("act", lambda e, ang=ang, tab=tab: e.activation(out=tab, in_=ang, func=AF.Sin),
                         reads=[A], writes=["tab%d" % kind])
                    P.op("sp", lambda e, tab=tab, kind=kind, pc=pc: e.dma_start(out=cs_d[kind, :, pc * 1024:(pc + 1) * 1024], in_=tab),
                         reads=["tab%d" % kind], writes=["cs_d"], dma=True)

        def rmsnorm_T(gidx, ntok_sub, src_res):
            norm_scale(ntok_sub)
            norm_transposes(gidx, ntok_sub)

        def norm_scale(ntok_sub):
            for s in range(ntok_sub):
                P.op("act", lambda e, s=s: e.activation(out=hb[:, s, :], in_=xt[:, s, :], func=AF.Square, accum_out=ss[:, s:s + 1]),
                     reads=["xt%d" % s], writes=["HY", "ss%d" % s])
                P.op("act", lambda e, s=s: e.activation(out=rs[:, s:s + 1], in_=ss[:, s:s + 1], func=AF.Sqrt, scale=1.0 / D, bias=epsb[:]),
                     reads=["ss%d" % s, "epsb"], writes=["rs%d" % s])
                P.op("dve", lambda e, s=s: e.reciprocal(out=rs[:, s:s + 1], in_=rs[:, s:s + 1]), reads=["rs%d" % s], writes=["rs%d" % s])
                P.op("dve", lambda e, s=s: e.tensor_scalar(out=hb[:, s, :], in0=xt[:, s, :], scalar1=rs[:, s:s + 1], scalar2=None, op0=ALU.mult),
                     reads=["xt%d" % s, "rs%d" % s], writes=["HY"])

        def norm_transposes(gidx, ntok_sub):
            n = ntok_sub * 128
            for c in range(16):
                bank, bres = PT.next()
                for s in range(ntok_sub):
                    P.op("pe", lambda e, s=s, c=c, bank=bank: e.transpose(out=bank[:, s * 128:(s + 1) * 128], in_=hb[:, s, c * 128:(c + 1) * 128], identity=ident[:]),
                         reads=["HY", "ident"], writes=[bres])
                if c % 2 == 0:
                    P.op("act", lambda e, c=c, bank=bank: e.activation(out=hT[:, c, 0:n], in_=bank[:, 0:n], func=AF.Copy, scale=gT[:, gidx, c:c + 1]),
                         reads=[bres, "gT"], writes=["hT"])
                else:
                    P.op("dve", lambda e, c=c, bank=bank: e.tensor_scalar(out=hT[:, c, 0:n], in0=bank[:, 0:n], scalar1=gT[:, gidx, c:c + 1], scalar2=None, op0=ALU.mult),
                         reads=[bres, "gT"], writes=["hT"])

        def proj_fm(wname, cb, n, sink):
            slot, sres = W.next(wname, cb)
            for j in range(4):
                pa, pres = PA.next()
                for kc in range(16):
                    mm(pa[:, 0:n], slot[:, kc, j * 128:(j + 1) * 128], hT[:, kc, 0:n], kc == 0, kc == 15, [sres, "hT"], [pres])
                sink(j, pa, pres)

        def mem_kv(layer):
            for s in range(2):
                P.op("sp", lambda e, s=s: e.dma_start(out=xt[:, s, :], in_=mem_d[s * 128:(s + 1) * 128, :]), writes=["xt%d" % s], dma=True)
            rmsnorm_T(3 + layer, 2, None)
            wname = "w_mkv%d" % layer

            def sink_k(j, pa, pres):
                evac("copy", mkT[:, j, :], pa[:, 0:256], [pres], ["mkT"], eng="dve" if j % 2 else None)
            proj_fm(wname, 0, 256, sink_k)
            slot, sres = W.next(wname, 1)
            for mc in range(2):
                pa, pres = PA.next()
                for kc in range(16):
                    mm(pa[:, :], hT[:, kc, mc * 128:(mc + 1) * 128], slot[:, kc, :], kc == 0, kc == 15, [sres, "hT"], [pres])
                evac("copy", mv[:, mc, :], pa[:, :], [pres], ["mv"])

        def mem_attn(n, ybase):
            for h in range(4):
                for mc in range(2):
                    pa, pres = PA.next()
                    mm(pa[:, 0:n], mkT[:, h, mc * 128:(mc + 1) * 128], qmT[:, h, 0:n], True, True, ["mkT", "qmT"], [pres])
                    evac("exp", pexp[:, mc, 0:n], pa[:, 0:n], [pres], ["pexp%d" % mc], scale=SCALE)
                po, pores = PA.next()
                for mc in range(2):
                    mm(po[:, 0:n], mv[:, mc, h * 128:(h + 1) * 128], pexp[:, mc, 0:n], mc == 0, mc == 1, ["mv", "pexp%d" % mc], [pores])
                pz, pzres = PA.next()
                for mc in range(2):
                    mm(pz[:, 0:n], ones[:], pexp[:, mc, 0:n], mc == 0, mc == 1, ["ones", "pexp%d" % mc], [pzres])
                P.op("dve", lambda e, pz=pz: e.reciprocal(out=rzb[:, 0:n], in_=pz[:, 0:n]), reads=[pzres], writes=["rzb"])
                P.op("dve", lambda e, po=po: e.tensor_tensor(out=tm2[:, 0:n], in0=po[:, 0:n], in1=rzb[:, 0:n], op=ALU.mult), reads=[pores, "rzb"], writes=["tm2"])
                P.op("dve", lambda e, h=h: e.tensor_tensor(out=yT[:, ybase + h, 0:n], in0=tm2[:, 0:n], in1=szm[:, h, 0:n], op=ALU.mult),
                     reads=["tm2", "szm"], writes=["HY"])

        def out_proj(wname, nsub, n_cb=4):
            for cb in range(n_cb):
                slot, sres = W.next(wname, cb)
                for s in range(nsub):
                    pa, pres = PA.next()
                    for kc in range(16):
                        mm(pa[:, :], yT[:, kc, s * 128:(s + 1) * 128], slot[:, kc, :], kc == 0, kc == 15, [sres, "HY"], [pres])
                    P.op("dve", lambda e, s=s, cb=cb, pa=pa: e.tensor_tensor(out=xt[:, s, cb * 512:(cb + 1) * 512], in0=pa[:, :], in1=xt[:, s, cb * 512:(cb + 1) * 512], op=ALU.add),
                         reads=[pres, "xt%d" % s], writes=["xt%d" % s])

        def layer0_loadnorm(tt):
            T0 = tt * 512
            for s in range(4):
                P.op("sp", lambda e, s=s: e.dma_start(out=xt[:, s, :], in_=x_d[T0 + s * 128:T0 + (s + 1) * 128, :]), writes=["xt%d" % s], dma=True)
            norm_scale(4)

        def layer0_tile(tt):
            T0 = tt * 512
            P.op("sp", lambda e: e.dma_start(out=cst[0:32, :, :], in_=cs_d[:, :, T0:T0 + 512].rearrange("k p n -> p k n")), reads=["cs_d"], writes=["cst"], dma=True)
            norm_transposes(0, 4)
            if tt == 0:
                P.op("pool", lambda e: e.memset(uT[:, :, 0:16], 0.0), writes=["uT"])
            elif tt == 4:
                P.op("pool", lambda e: e.tensor_scalar(out=uT[:, :, 0:16], in0=uT[:, :, 512:528], scalar1=hflag[:, 0:1], scalar2=None, op0=ALU.mult),
                     reads=["uT", "hflag"], writes=["uT"])
            else:
                P.op("pool", lambda e: e.tensor_copy(out=uT[:, :, 0:16], in_=uT[:, :, 512:528]), reads=["uT"], writes=["uT"])

            def sink_zu(cb):
                def f(j, pa, pres):
                    c = (cb - 3) * 4 + j
                    evac("silu", szu[:, c, :], pa[:, :], [pres], ["szu"])
                return f

            def sink_u(cb):
                def f(j, pa, pres):
                    c = cb * 4 + j
                    evac("copy", uT[:, c, 16:528], pa[:, :], [pres], ["uT"], eng="dve" if j % 2 else None)
                return f

            def sink_qm(j, pa, pres):
                evac("copy", qmT[:, j, :], pa[:, :], [pres], ["qmT"], eng="dve")

            def sink_zm(j, pa, pres):
                evac("silu", szm[:, j, :], pa[:, :], [pres], ["szm"])

            for cb in (3, 4, 5):
                proj_fm("w_in0", cb, 512, sink_zu(cb))
            for cb in (0, 1, 2):
                proj_fm("w_in0", cb, 512, sink_u(cb))
            proj_fm("w_in0", 6, 512, sink_qm)
            proj_fm("w_in0", 7, 512, sink_zm)
            for g in range(4):
                U = uT[:, 3 * g:3 * g + 3, :]
                win = WINS[g]
                cur = U
                curres = "uT"
                lvl = 0
                sh = 1
                bufs = [(ta, "ta"), (tb, "tb")]
                while sh < win:
                    dst, dres = bufs[lvl % 2]
                    lo = 2 * sh
                    P.op("pool", lambda e, dst=dst, cur=cur, lo=lo, sh=sh: e.tensor_tensor(out=dst[:, :, lo:528], in0=cur[:, :, lo:528], in1=cur[:, :, lo - sh:528 - sh], op=ALU.add),
                         reads=[curres], writes=[dres])
                    cur, curres = dst, dres
                    lvl += 1
                    sh *= 2
                pb = poolb[g % 2]
                pbres = "poolb%d" % (g % 2)
                P.op("dve", lambda e, pb=pb, cur=cur, U=U, win=win: e.scalar_tensor_tensor(out=pb[:, :, :], in0=cur[:, :, 16:528], scalar=1.0 / win, in1=U[:, :, 16:528], op0=ALU.mult, op1=ALU.subtract),
                     reads=[curres, "uT"], writes=[pbres])
                if tt in (0, 4):
                    kind = 0 if tt == 0 else 1
                    ic = bass.AP(cpool[:, kind, g, :].tensor, cpool[:, kind, g, :].offset, [list(cpool[:, kind, g, :].ap[0]), [0, 3], [1, 16]])
                    P.op("dve", lambda e, cur=cur, ic=ic: e.tensor_tensor(out=cur[:, :, 0:16], in0=cur[:, :, 16:32], in1=ic, op=ALU.mult),
                         reads=[curres, "cpool"], writes=[curres])
                    P.op("dve", lambda e, cur=cur, pb=pb, U=U: e.tensor_tensor(out=pb[:, :, 0:16], in0=cur[:, :, 0:16], in1=U[:, :, 16:32], op=ALU.subtract),
                         reads=[curres, "uT"], writes=[pbres])
                for oc in range(3):
                    pa, pres = PA.next()
                    for kc in range(3):
                        mm(pa[:, :], wpool[:, g * 3 + kc, oc * 128:(oc + 1) * 128], pb[:, kc, :], kc == 0, kc == 2, ["wpool", pbres], [pres])
                    c = g * 3 + oc
                    P.op("dve", lambda e, pa=pa, c=c: e.scalar_tensor_tensor(out=yT[:, c, :], in0=pa[:, :], scalar=pscale[:, c:c + 1], in1=szu[:, c, :], op0=ALU.mult, op1=ALU.mult),
                         reads=[pres, "pscale", "szu"], writes=["HY"])
            mem_attn(512, 12)
            out_proj("w_out0", 4)
            if tt >= 4:
                for s in range(4):
                    r0 = (tt - 4) * 512 + s * 128
                    P.op("pool", lambda e, s=s, r0=r0: e.dma_start(out=h1_d[r0:r0 + 128, :], in_=xt[:, s, :]), reads=["xt%d" % s], writes=["h1_d"], dma=True)
                    if "h1" in dbg:
                        P.op("pool", lambda e, s=s, r0=r0: e.dma_start(out=dbg_out["h1"][r0:r0 + 128, :], in_=xt[:, s, :]), reads=["xt%d" % s], writes=["dbg_h1"], dma=True)
            rmsnorm_T(2, 4, None)
            if tt + 1 < NT0:
                layer0_loadnorm(tt + 1)

            def sink_plain(dst_d):
                def f(j, pa, pres):
                    st, stres = KST.next()
                    evac("copy", st[:, :], pa[:, :], [pres], [stres], eng="dve" if j % 2 else None)
                    P.op("pool", lambda e, st=st, j=j: e.dma_start(out=dst_d[j, :, T0:T0 + 512], in_=st[:, :]), reads=[stres], writes=[DN[id(dst_d)]], dma=True)
                return f

            def sink_rope(dst_d):
                def f(j, pa, pres):
                    st, stres = KST.next()
                    evac("copy", st[:, :], pa[:, :], [pres], [stres])
                    pr, prres = PA.next()
                    mm(pr[:, :], rperm[:, :], st[:, :], True, True, ["rperm", stres], [prres])
                    r1 = FA[0:32, 18720:18720 + 512]
                    r2 = FA[0:32, 17696:17696 + 512]
                    P.op("dve", lambda e, st=st, r1=r1: e.tensor_tensor(out=r1, in0=st[0:32, :], in1=cst[0:32, 0, :], op=ALU.mult), reads=[stres, "cst"], writes=["r1"])
                    P.op("dve", lambda e, pr=pr, r2=r2: e.tensor_tensor(out=r2, in0=pr[0:32, :], in1=cst[0:32, 1, :], op=ALU.mult), reads=[prres, "cst"], writes=["rzb"])
                    P.op("dve", lambda e, st=st, r1=r1, r2=r2: e.tensor_tensor(out=st[0:32, :], in0=r1, in1=r2, op=ALU.add), reads=["r1", "rzb"], writes=[stres])
                    P.op("pool", lambda e, st=st, j=j: e.dma_start(out=dst_d[j, :, T0:T0 + 512], in_=st[:, :]), reads=[stres], writes=[DN[id(dst_d)]], dma=True)
                return f

            proj_fm("w_kv", 0, 512, sink_plain(kcT_d))
            proj_fm("w_kv", 1, 512, sink_plain(vcT_d))
            proj_fm("w_kv", 2, 512, sink_rope(ksT_d))
            proj_fm("w_kv", 4, 512, sink_rope(kwT_d))
            for cb, dst_d in ((3, vs_d), (5, vw_d)):
                slot, sres = W.next("w_kv", cb)
                for s in range(4):
                    pa, pres = PA.next()
                    for kc in range(16):
                        mm(pa[:, :], hT[:, kc, s * 128:(s + 1) * 128], slot[:, kc, :], kc == 0, kc == 15, [sres, "hT"], [pres])
                    st, stres = KST.next()
                    evac("copy", st[:, :], pa[:, :], [pres], [stres], eng="dve" if s % 2 else None)
                    P.op("pool", lambda e, st=st, s=s, dst_d=dst_d: e.dma_start(out=dst_d[T0 + s * 128:T0 + (s + 1) * 128, :], in_=st[:, :]), reads=[stres], writes=[DN[id(dst_d)]], dma=True)

        def compress():
            cstab = BA[0:32, 16448:16448 + 8192].rearrange("p (k n) -> p k n", k=2)
            P.op("sp", lambda e: e.dma_start(out=cstab, in_=cs_d.rearrange("k p n -> p k n")), reads=["cs_d"], writes=["cstab"], dma=True)
            peb = BA[:, 24640:24640 + 68].rearrange("p (k l) -> p k l", k=2)
            pef = FA[:, 0:64].rearrange("p (k l) -> p k l", k=2)
            P.op("sp", lambda e: e.dma_start(out=pef, in_=peT_d), writes=["pef"], dma=True)
            P.op("dve", lambda e: e.memset(peb, 0.0), writes=["peb"])
            P.op("dve", lambda e: e.tensor_copy(out=peb[:, :, 0:32], in_=pef), reads=["pef", "peb"], writes=["peb"])
            bh = FA[:, 64:68].rearrange("p (k h) -> p k h", k=2)
            P.op("pool", lambda e: e.memset(kcmpT[:], 0.0), writes=["kcmpT"])
            P.op("pool", lambda e: e.memset(vcmpx[:], 0.0), writes=["vcmpx"])
            P.op("pool", lambda e: e.memset(vcmpx[:].rearrange("p a g d -> p (a g) d")[:, :, 192:193], 1.0), reads=["vcmpx"], writes=["vcmpx"])
            for nc_ in range(2):
                for g in range(4):
                    P.op("sp", lambda e, nc_=nc_, g=g: e.dma_start(out=vcmpx[:, nc_, g, 128:192], in_=ovl_d[nc_]), reads=["vcmpx"], writes=["vcmpx"], dma=True)
            zT = [BA[:, 0:4128], BA[:, 4128:8256]]
            w1s = BA[:, 8256:16448].rearrange("p (l h) -> p l h", l=32)
            w2s = BA[:, 24708:24708 + 256].rearrange("p (c d) -> p c d", c=2)
            hid = BA[:, 24964:24964 + 512].rearrange("p (c n) -> p c n", c=2)
            kst0 = BA[:, 25476:25476 + 256]
            for i_ in range(2):
                P.op("pool", lambda e, i_=i_: e.memset(zT[i_][:, 4096:4128], 0.0), writes=["zT%d" % i_])
            zi = 0
            for kv in range(2):
                P.op("sp", lambda e, kv=kv: e.dma_start(out=w1s, in_=wb["cw1"][kv * 4096:(kv + 1) * 4096, :].rearrange("(l p) h -> p l h", p=128)),
                     reads=["cw1_b"], writes=["w1s"], dma=True)
                P.op("sp", lambda e, kv=kv: e.dma_start(out=w2s, in_=wb["cw2"][kv * 256:(kv + 1) * 256, :].rearrange("(c p) d -> p c d", p=128)),
                     reads=["cw2_b"], writes=["w2s"], dma=True)
                for hc in range(2):
                    pa, pres = PA.next()
                    for l in range(32):
                        mm(pa[:, 0:2], w1s[:, l, hc * 128:(hc + 1) * 128], peb[:, kv, l:l + 2], l == 0, l == 31, ["w1s", "peb"], [pres])
                    P.op("dve", lambda e, pa=pa, kv=kv, hc=hc: e.tensor_copy(out=bh[:, kv, hc:hc + 1], in_=pa[:, 0:1]), reads=[pres], writes=["bh"])
                src_d = kcT_d if kv == 0 else vcT_d
                for g in range(4):
                    z = zT[zi % 2]
                    zres = "zT%d" % (zi % 2)
                    zi += 1
                    P.op("sp", lambda e, z=z, g=g, src_d=src_d: e.dma_start(out=z[:, 0:4096], in_=src_d[g]), reads=[DN[id(src_d)]], writes=[zres], dma=True)
                    for hc in range(2):
                        pa, pres = PA.next()
                        for l in range(32):
                            mm(pa[:, 0:256], w1s[:, l, hc * 128:(hc + 1) * 128], z[:, l:l + 16 * 255 + 1:16], l == 0, l == 31, ["w1s", zres], [pres])
                        P.op("act", lambda e, pa=pa, kv=kv, hc=hc: e.activation(out=hid[:, hc, 0:256], in_=pa[:, 0:256], func=AF.Silu, bias=bh[:, kv, hc:hc + 1]),
                             reads=[pres, "bh"], writes=["hid"])
                    if kv == 0:
                        pa, pres = PA.next()
                        for hc in range(2):
                            mm(pa[:, 0:256], w2s[:, hc, :], hid[:, hc, 0:256], hc == 0, hc == 1, ["w2s", "hid"], [pres])
                        evac("copy", kst0[:, 0:256], pa[:, 0:256], [pres], ["kst0"])
                        pr, prres = PA.next()
                        mm(pr[:, 0:256], rperm[:, :], kst0[:, 0:256], True, True, ["rperm", "kst0"], [prres])
                        r1 = FA[0:32, 18720:18720 + 255]
                        r2 = FA[0:32, 17696:17696 + 255]
                        P.op("dve", lambda e, r1=r1: e.tensor_tensor(out=r1, in0=kst0[0:32, 0:255], in1=cstab[:, 0, 31:31 + 16 * 254 + 1:16], op=ALU.mult), reads=["kst0", "cstab"], writes=["r1"])
                        P.op("dve", lambda e, pr=pr, r2=r2: e.tensor_tensor(out=r2, in0=pr[0:32, 0:255], in1=cstab[:, 1, 31:31 + 16 * 254 + 1:16], op=ALU.mult), reads=[prres, "cstab"], writes=["rzb"])
                        P.op("dve", lambda e, r1=r1, r2=r2: e.tensor_tensor(out=kst0[0:32, 0:255], in0=r1, in1=r2, op=ALU.add), reads=["r1", "rzb"], writes=["kst0"])
                        P.op("dve", lambda e, g=g: e.tensor_copy(out=kcmpT[:, g, 0:256], in_=kst0[:, 0:256]), reads=["kst0"], writes=["kcmpT"])
                    else:
                        for nc_ in range(2):
                            nn = 128
                            pa, pres = PA.next()
                            for hc in range(2):
                                mm(pa[0:nn, 0:128], hid[:, hc, nc_ * 128:nc_ * 128 + nn], w2s[:, hc, :], hc == 0, hc == 1, ["hid", "w2s"], [pres])
                            evac("copy", vcmpx[0:nn, nc_, g, 0:128], pa[0:nn, 0:128], [pres], ["vcmpx"], eng="dve" if nc_ else None)

        FB1 = FA[:, 4096:8192].bitcast(BF16)
        FB2 = FA[:, 13440:17696].bitcast(BF16)
        Es = FB1[:, 0:4096]
        ksg = [FB1[:, 4096 + i * 512:4096 + (i + 1) * 512] for i in range(2)]
        vsg = [FB1[:, 5120 + i * 520:5120 + (i + 1) * 520].rearrange("p (c d) -> p c d", c=4) for i in range(2)]
        ksg.append(BA[:, 33280:33792])
        vsg.append(BA[:, 33792:34312].rearrange("p (c d) -> p c d", c=4))
        ksg.append(BA[:, 35602:36114])
        vsg.append(BA[:, 36114:36634].rearrange("p (c d) -> p c d", c=4))
        kwg = [FB1[:, 6160:6800], BA[:, 34312:34952]]
        vwg = [FB1[:, 6800:7450].rearrange("p (c d) -> p c d", c=5), BA[:, 34952:35602].rearrange("p (c d) -> p c d", c=5)]
        selT = FB1[:, 7456:7840]
        ybf = FB2[:, 0:1536]
        pcm = [FB2[:, 1536 + i * 384:1536 + (i + 1) * 384] for i in range(2)]
        pse = [FB2[:, 2304 + i * 384:2304 + (i + 1) * 384] for i in range(3)]
        CBs = FB2[:, 3456:3968].rearrange("p (c n) -> p c n", c=2)
        WBs = FB2[:, 3968:4480].rearrange("p (c n) -> p c n", c=4)
        wgl = FB2[:, 4480:5056].rearrange("p (c n) -> p c n", c=16)
        selb = FB2[:, 5056:5184]
        gfin = FA[:, 8192:10240]
        M1s = FA[:, 10240:11264].rearrange("p (i j) -> p i j", i=16)
        Cs = FA[:, 11264:12288].rearrange("p (i j) -> p i j", i=16)
        gates = FA[:, 12288:12360].rearrange("p (s j) -> p s j", s=2)
        yacc = FA[:, 12400:12784]
        YACC = [yacc, FA[:, 13040:13424]]
        imp = FA[:, 12800:12864]
        sc = FA[:, 12864:12928]
        sc2 = FA[:, 12928:12992]
        m8 = FA[:, 12992:13008]
        zc = FA[:, 13008:13011]
        rcc = FA[:, 13012:13015]
        w0 = FA[:, 13016:13019]
        thr = FA[:, 13020:13021]
        rcw = FA[:, 13024:13027]
        w1_ = FA[:, 13028:13031]
        szq = szu
        qT = BA[:, 26624:29696].rearrange("p (s c n) -> p s c n", s=2, c=12)
        PS2 = Rot([(pA[0], "pA0"), (pA[1], "pA1")])
        PSE = Rot([(pse[i], "pse%d" % i) for i in range(3)])

        def l1_setup():
            P.op("sp", lambda e: e.dma_start(out=Es, in_=E_d), writes=["Es"], dma=True)
            P.op("pool", lambda e: e.memset(selT, 0.0), writes=["selT"])
            P.op("pool", lambda e: e.memset(selb, 0.0), writes=["selb"])
            P.op("sp", lambda e: e.dma_start(out=WBs, in_=WB_d), writes=["WBs"], dma=True)
            P.op("sp", lambda e: e.dma_start(out=wgl, in_=wb["w_gl"].rearrange("(c p) n -> p c n", p=128)), reads=["w_gl_b"], writes=["wgl"], dma=True)
            P.op("sp", lambda e: e.dma_start(out=gfin, in_=gfin_d.to_broadcast([128, D])), writes=["gfin"], dma=True)
            P.op("sp", lambda e: e.dma_start(out=M1s, in_=M1_d.rearrange("(i p) j -> p i j", p=128)), writes=["M1s"], dma=True)
            P.op("sp", lambda e: e.dma_start(out=Cs, in_=C_d.rearrange("(i p) j -> p i j", p=128)), writes=["Cs"], dma=True)
            for i in range(4):
                P.op("pool", lambda e, i=i: e.memset(vsg[i][:, :, 128:130], 0.0), writes=["vsg%d" % i])
                P.op("pool", lambda e, i=i: e.memset(vsg[i][:, :, 128:129], 1.0), reads=["vsg%d" % i], writes=["vsg%d" % i])
            for i in range(2):
                P.op("pool", lambda e, i=i: e.memset(vwg[i][:, :, 128:130], 0.0), writes=["vwg%d" % i])
                P.op("pool", lambda e, i=i: e.memset(vwg[i][:, :, 128:129], 1.0), reads=["vwg%d" % i], writes=["vwg%d" % i])

        def att_cmp(i, s, g):
            qv = qT[:, s, 3 * g:3 * g + 3, :].rearrange("p a b -> p (a b)")
            as3 = lambda ap: ap
            gv = gates[:, s, 9 * g:9 * g + 9].rearrange("p (r k) -> p r k", r=3)
            for nc_ in range(2):
                pa, pres = PS2.next()
                mm(as3(pa[:, 0:384]), kcmpT[:, g, nc_ * 128:(nc_ + 1) * 128], qv, True, False, ["kcmpT", "qT"], [pres])
                mm(pa[:, 0:384].rearrange("p (a b) -> p a b", a=3), ident[:], bc(CBs[:, nc_, s * 128:(s + 1) * 128], 3), False, True, ["ident", "CBs"], [pres])
                evac("exp", pcm[nc_], pa[:, 0:384], [pres], ["pcm%d" % nc_], scale=SCALE)
            for r in range(3):
                for nc_ in range(2):
                    mm(pO[:, r * 256:r * 256 + 194], pcm[nc_][:, r * 128:(r + 1) * 128], vcmpx[:, nc_, g, :], nc_ == 0, nc_ == 1, ["pcm%d" % nc_, "vcmpx"], ["pO"])
            P.op("dve", lambda e: e.tensor_scalar(out=zc, in0=pO[:, 192:192 + 2 * 256 + 1:256], scalar1=1e-30, scalar2=None, op0=ALU.add), reads=["pO"], writes=["zc"])
            P.op("dve", lambda e: e.reciprocal(out=rcc, in_=zc), reads=["zc"], writes=["rcc"])
            P.op("dve", lambda e: e.tensor_scalar(out=imp, in0=pO[:, 128:192], scalar1=rcc[:, 0:1], scalar2=None, op0=ALU.mult), reads=["pO", "rcc"], writes=["imp"])
            for r in (1, 2):
                P.op("dve", lambda e, r=r: e.scalar_tensor_tensor(out=imp, in0=pO[:, r * 256 + 128:r * 256 + 192], scalar=rcc[:, r:r + 1], in1=imp, op0=ALU.mult, op1=ALU.add),
                     reads=["pO", "rcc", "imp"], writes=["imp"])
            P.op("dve", lambda e: e.tensor_tensor(out=w0, in0=rcc, in1=gv[:, :, 0], op=ALU.mult), reads=["rcc", "gates"], writes=["w0"])
            for r in range(3):
                P.op("dve", lambda e, r=r: e.tensor_scalar(out=YACC[g % 2][:, r * 128:(r + 1) * 128], in0=pO[:, r * 256:r * 256 + 128], scalar1=w0[:, r:r + 1], scalar2=None, op0=ALU.mult),
                     reads=["pO", "w0"], writes=["yacc%d" % (g % 2)])
            P.op("dve", lambda e: e.tensor_tensor(out=sc, in0=imp, in1=M1s[:, i, :], op=ALU.mult), reads=["imp", "M1s"], writes=["sc"])
            P.op("dve", lambda e: e.tensor_tensor(out=sc, in0=sc, in1=Cs[:, i, :], op=ALU.add), reads=["sc", "Cs"], writes=["sc"])
            P.op("dve", lambda e: e.max(out=m8[:, 0:8], in_=sc), reads=["sc"], writes=["m8"])
            P.op("dve", lambda e: e.match_replace(out=sc2, in_to_replace=m8[:, 0:8], in_values=sc, imm_value=-1e30), reads=["sc", "m8"], writes=["sc2"])
            P.op("dve", lambda e: e.max(out=m8[:, 8:16], in_=sc2), reads=["sc2"], writes=["m8"])
            P.op("dve", lambda e: e.tensor_scalar(out=thr, in0=m8[:, 15:16], scalar1=-1e29, scalar2=None, op0=ALU.max), reads=["m8"], writes=["thr"])
            P.op("dve", lambda e: e.tensor_scalar(out=sc2, in0=sc, scalar1=thr[:, 0:1], scalar2=None, op0=ALU.is_ge), reads=["sc", "thr"], writes=["sc2"])
            P.op("dve", lambda e: e.tensor_scalar(out=selb[:, 0:64], in0=sc2, scalar1=1.0, scalar2=-NEG, op0=ALU.subtract, op1=ALU.mult), reads=["sc2"], writes=["selb"])

        def attention(i, s, g, first, nxt):
            qv = qT[:, s, 3 * g:3 * g + 3, :].rearrange("p a b -> p (a b)")
            as3 = lambda ap: ap
            gv = gates[:, s, 9 * g:9 * g + 9].rearrange("p (r k) -> p r k", r=3)
            yacc = YACC[g % 2]
            YR = "yacc%d" % (g % 2)
            if first:
                att_cmp(i, s, g)
            def wload(bw):
                P.op("sp", lambda e: e.dma_start(out=kwg[bw], in_=kwT_d[g, :, (12 + i) * 128:(17 + i) * 128]), reads=["kwT_d"], writes=["kwg%d" % bw], dma=True)
                P.op("sp", lambda e: e.dma_start(out=vwg[bw][:, :, 0:128], in_=vw_d[(12 + i) * 128:(17 + i) * 128, g * 128:(g + 1) * 128].rearrange("(c p) d -> p c d", p=128)),
                     reads=["vw_d"], writes=["vwg%d" % bw], dma=True)
            bw = KVW.next(wload)
            jobs = []
            for d in range(5):
                c = 12 + i + d
                if c < 16:
                    pat = 0 if d == 0 else 1
                else:
                    pat = 2 if d == 0 else (3 if d == 4 else None)
                mms = [(kwg[bw][:, d * 128:(d + 1) * 128], qv, ["kwg%d" % bw, "qT"])]
                if pat is not None:
                    mms.append((ident[:], bc(WBs[:, pat, :], 3), ["ident", "WBs"]))
                jobs.append(dict(mms=mms, v=vwg[bw][:, d, :], vres="vwg%d" % bw, acc=pA[3], accres="pA3", first=d == 0, last=d == 4, sel=False))
            nch = 17 + i
            ncg = (nch + 3) // 4

            def mk_loader(cg, n):
                def f(b):
                    P.op("sp", lambda e: e.dma_start(out=ksg[b][:, 0:n * 128], in_=ksT_d[g, :, cg * 512:cg * 512 + n * 128]), reads=["ksT_d"], writes=["ksg%d" % b], dma=True)
                    P.op("sp", lambda e: e.dma_start(out=vsg[b][:, 0:n, 0:128], in_=vs_d[cg * 512:cg * 512 + n * 128, g * 128:(g + 1) * 128].rearrange("(c p) d -> p c d", p=128)),
                         reads=["vs_d"], writes=["vsg%d" % b], dma=True)
                return f
            for cg in range(ncg):
                ncc = min(4, nch - cg * 4)
                jobs.append(dict(load=mk_loader(cg, ncc)))
                for cc in range(ncc):
                    c = cg * 4 + cc
                    last = (c == nch - 1)
                    jobs.append(dict(kc=(cc, c), last=last, first=c == 0, sel=True))

            cur_b = [None]

            def stageA(job):
                if job["sel"]:
                    b = cur_b[0]
                    cc, c = job["kc"]
                    mms = [(ksg[b][:, cc * 128:(cc + 1) * 128], qv, ["ksg%d" % b, "qT"]),
                           (Es[:, c * 128:(c + 1) * 128], selT, ["Es", "selT"])]
                    if job["last"]:
                        mms.append((ident[:], bc(WBs[:, 3, :], 3), ["ident", "WBs"]))
                    job.update(mms=mms, v=vsg[b][:, cc, :], vres="vsg%d" % b, acc=pA[2], accres="pA2")
                pa, pres = PS2.next()
                n = len(job["mms"])
                for k, (l, r_, rd) in enumerate(job["mms"]):
                    o_ = pa[:, 0:384].rearrange("p (a b) -> p a b", a=3) if len(r_.shape) == 3 else pa[:, 0:384]
                    mm(o_, l, r_, k == 0, k == n - 1, rd, [pres])
                pe_, peres = PSE.next()
                evac("exp", pe_, pa[:, 0:384], [pres], [peres], scale=SCALE)
                job["pe"] = (pe_, peres)

            def stageB(job):
                pe_, peres = job["pe"]
                for r in range(3):
                    mm(job["acc"][:, r * 130:(r + 1) * 130], pe_[:, r * 128:(r + 1) * 128], job["v"], job["first"], job["last"], [peres, job["vres"]], [job["accres"]])

            def win_epilogue():
                P.op("dve", lambda e: e.reciprocal(out=rcw, in_=pA[3][:, 128:128 + 2 * 130 + 1:130]), reads=["pA3"], writes=["rcw"])
                P.op("dve", lambda e: e.tensor_tensor(out=w1_, in0=rcw, in1=gv[:, :, 2], op=ALU.mult), reads=["rcw", "gates"], writes=["w1_"])
                for r in range(3):
                    P.op("dve", lambda e, r=r: e.scalar_tensor_tensor(out=yacc[:, r * 128:(r + 1) * 128], in0=pA[3][:, r * 130:r * 130 + 128], scalar=w1_[:, r:r + 1], in1=yacc[:, r * 128:(r + 1) * 128], op0=ALU.mult, op1=ALU.add),
                         reads=["pA3", "w1_", YR], writes=[YR])

            prev = None
            transposed = False
            for job in jobs:
                if "load" in job:
                    cur_b[0] = KVS.next(job["load"])
                    continue
                if job["sel"] and not transposed:
                    bank, bres = PT.next()
                    P.op("pe", lambda e, bank=bank: e.transpose(out=bank[:, 0:128], in_=selb, identity=ident[:]), reads=["selb", "ident"], writes=[bres])
                    P.op("act", lambda e, bank=bank: e.copy(out=selT[0:64, :].rearrange("p (a b) -> p a b", a=3), in_=bc(bank[0:64, 0:128], 3)), [bres], ["selT"])
                    transposed = True
                    if nxt:
                        att_cmp(i, s, g + 1)
                stageA(job)
                if prev is not None:
                    stageB(prev)
                    if (not prev["sel"]) and prev["last"]:
                        win_epilogue()
                prev = job
            stageB(prev)
            P.op("dve", lambda e: e.reciprocal(out=rcc, in_=pA[2][:, 128:128 + 2 * 130 + 1:130]), reads=["pA2"], writes=["rcc"])
            P.op("dve", lambda e: e.tensor_tensor(out=w0, in0=rcc, in1=gv[:, :, 1], op=ALU.mult), reads=["rcc", "gates"], writes=["w0"])
            for r in range(3):
                h = 3 * g + r
                P.op("dve", lambda e, r=r, h=h: e.scalar_tensor_tensor(out=ybf[:, h * 128:(h + 1) * 128], in0=pA[2][:, r * 130:r * 130 + 128], scalar=w0[:, r:r + 1], in1=yacc[:, r * 128:(r + 1) * 128], op0=ALU.mult, op1=ALU.add),
                     reads=["pA2", "w0", YR], writes=["ybf"])

        def layer1_tile(tt):
            r0 = tt * 256
            T0 = HALF + r0
            for s in range(2):
                P.op("sp", lambda e, s=s: e.dma_start(out=xt[:, s, :], in_=h1_d[r0 + s * 128:r0 + (s + 1) * 128, :]), reads=["h1_d"], writes=["xt%d" % s], dma=True)
            P.op("sp", lambda e: e.dma_start(out=cst[0:32, :, 0:256], in_=cs_d[:, :, T0:T0 + 256].rearrange("k p n -> p k n")), reads=["cs_d"], writes=["cst"], dma=True)
            P.op("sp", lambda e: e.dma_start(out=CBs, in_=CB_d[:, :, r0:r0 + 256].rearrange("c p n -> p c n")), writes=["CBs"], dma=True)
            rmsnorm_T(1, 2, None)

            def sink_q(cb):
                def f(j, pa, pres):
                    h = cb * 4 + j
                    st, stres = KST.next()
                    evac("copy", st[:, 0:256], pa[:, 0:256], [pres], [stres])
                    pr, prres = PA.next()
                    mm(pr[:, 0:256], rperm[:, :], st[:, 0:256], True, True, ["rperm", stres], [prres])
                    r1 = FA[0:32, 18720:18720 + 256]
                    r2 = FA[0:32, 17696:17696 + 256]
                    P.op("dve", lambda e, st=st, r1=r1: e.tensor_tensor(out=r1, in0=st[0:32, 0:256], in1=cst[0:32, 0, 0:256], op=ALU.mult), reads=[stres, "cst"], writes=["r1"])
                    P.op("dve", lambda e, pr=pr, r2=r2: e.tensor_tensor(out=r2, in0=pr[0:32, 0:256], in1=cst[0:32, 1, 0:256], op=ALU.mult), reads=[prres, "cst"], writes=["rzb"])
                    P.op("dve", lambda e, st=st, r1=r1, r2=r2: e.tensor_tensor(out=st[0:32, 0:256], in0=r1, in1=r2, op=ALU.add), reads=["r1", "rzb"], writes=[stres])
                    P.op("pool", lambda e, st=st, h=h: e.tensor_copy(out=qT[:, :, h, :], in_=st[:, 0:256].rearrange("p (s n) -> p s n", s=2)), reads=[stres], writes=["qT"])
                return f

            def sink_zq(cb):
                def f(j, pa, pres):
                    evac("silu", szq[:, (cb - 3) * 4 + j, 0:256], pa[:, 0:256], [pres], ["szu"])
                return f

            def sink_qm(j, pa, pres):
                evac("copy", qmT[:, j, 0:256], pa[:, 0:256], [pres], ["qmT"], eng="dve")

            def sink_zm(j, pa, pres):
                evac("silu", szm[:, j, 0:256], pa[:, 0:256], [pres], ["szm"])

            for cb in (0, 1, 2):
                proj_fm("w_in1", cb, 256, sink_q(cb))
            for cb in (3, 4, 5):
                proj_fm("w_in1", cb, 256, sink_zq(cb))
            proj_fm("w_in1", 6, 256, sink_qm)
            proj_fm("w_in1", 7, 256, sink_zm)
            for s in range(2):
                pa, pres = PA.next()
                for kc in range(16):
                    mm(pa[:, 0:36], hT[:, kc, s * 128:(s + 1) * 128], wgl[:, kc, :], kc == 0, kc == 15, ["hT", "wgl"], [pres])
                P.op("act", lambda e, s=s, pa=pa: e.activation(out=gates[:, s, :], in_=pa[:, 0:36], func=AF.Sigmoid), reads=[pres], writes=["gates"])
            mem_attn(256, 12)
            for s in range(2):
                i = 2 * tt + s
                for g in range(4):
                    attention(i, s, g, g == 0, g < 3)
                for c in range(12):
                    bank, bres = PT.next()
                    P.op("pe", lambda e, c=c, bank=bank: e.transpose(out=bank[:, 0:128], in_=ybf[:, c * 128:(c + 1) * 128], identity=ident[:]), reads=["ybf", "ident"], writes=[bres])
                    P.op("dve", lambda e, c=c, s=s, bank=bank: e.tensor_tensor(out=yT[:, c, s * 128:(s + 1) * 128], in0=bank[:, 0:128], in1=szq[:, c, s * 128:(s + 1) * 128], op=ALU.mult),
                         reads=[bres, "szu"], writes=["HY"])
            out_proj("w_out1", 2)
            for s in range(2):
                P.op("act", lambda e, s=s: e.activation(out=hb[:, s, :], in_=xt[:, s, :], func=AF.Square, accum_out=ss[:, s:s + 1]),
                     reads=["xt%d" % s], writes=["HY", "ss%d" % s])
                P.op("act", lambda e, s=s: e.activation(out=rs[:, s:s + 1], in_=ss[:, s:s + 1], func=AF.Sqrt, scale=1.0 / D, bias=epsb[:]),
                     reads=["ss%d" % s, "epsb"], writes=["rs%d" % s])
                P.op("dve", lambda e, s=s: e.reciprocal(out=rs[:, s:s + 1], in_=rs[:, s:s + 1]), reads=["rs%d" % s], writes=["rs%d" % s])
                P.op("dve", lambda e, s=s: e.scalar_tensor_tensor(out=xt[:, s, :], in0=xt[:, s, :], scalar=rs[:, s:s + 1], in1=gfin, op0=ALU.mult, op1=ALU.mult),
                     reads=["xt%d" % s, "rs%d" % s, "gfin"], writes=["xt%d" % s])
                P.op("sp", lambda e, s=s: e.dma_start(out=out_d[r0 + s * 128:r0 + (s + 1) * 128, :], in_=xt[:, s, :]), reads=["xt%d" % s], writes=["out_d"], dma=True)

        def program():
            setup()
            P.barrier(skip=("pool",))
            mem_kv(0)
            layer0_loadnorm(0)
            for tt in range(NT0):
                layer0_tile(tt)
            P.barrier()
            if NT0 == 8 and "nol1" not in dbg:
                compress()
                P.barrier()
                l1_setup()
                mem_kv(1)
                for tt in range(NT1):
                    layer1_tile(tt)
            if "kv" in dbg:
                for nm, src in (("ksT", ksT_d), ("kcT", kcT_d), ("kwT", kwT_d)):
                    for g in range(4):
                        P.op("sp", lambda e, nm=nm, src=src, g=g: e.dma_start(out=dbg_out[nm][g], in_=src[g]), writes=["dbg_" + nm], dma=True)
                P.op("sp", lambda e: e.dma_start(out=dbg_out["vs"], in_=vs_d), writes=["dbg_vs"], dma=True)

        NT0 = int(os.environ.get("K_NT0", "8"))
        NT1 = int(os.environ.get("K_NT1", "8"))
        P.dry = True
        program()
        P.dry = False
        W.n = 0
        PA.i = PT.i = KST.i = PS2.i = PSE.i = 0
        KVS.n = KVW.n = 0
        program()
        P.emit(nc, es)
        if os.environ.get('K_STATS'):
            print('ops', {e: sum(1 for o in P.ops if o.eng == e) for e in ENGS}, 'waits', sum(len(o.waits) for o in P.ops), 'sigs', sum(1 for o in P.ops if o.sig))
    return nc


def _bf(a):
    return np.asarray(a, dtype=np.float32).astype(ml_dtypes.bfloat16)


def host_consts(p):
    c = {}
    c["ident"] = _bf(np.eye(128))
    rp = np.zeros((128, 128), np.float32)
    for i in range(32):
        rp[i + 16 if i < 16 else i - 16, i] = 1.0
    c["rperm"] = _bf(rp)
    cp = np.zeros((128, 2, 4, 16), np.float32)
    t1 = np.arange(1, 17, dtype=np.float32)
    for g, w in enumerate(WINS):
        seq = 1.0 / np.minimum(t1, float(w))
        cp[:, 0, g, :] = seq
        cp[:, 1, g, :] = seq if p == 0 else 1.0 / w
    c["cpool"] = cp
    c["hflag"] = np.full((128, 1), float(p), np.float32)
    half = 16
    inv = (500000.0 ** (-np.arange(half, dtype=np.float32) * 2.0 / 32)).astype(np.float32)
    rc = np.zeros((32, 4), np.float32)
    rc[:, 0] = np.concatenate([inv, inv])
    rc[:, 1] = 0.5 * np.pi
    rc[:16, 2] = np.pi
    rc[16:, 2] = 0.0
    c["ropec"] = rc
    own0 = 2048 * p
    tq = own0 + np.arange(HALF)
    sstart = np.where(np.arange(64) < 32, (1 - p) * 2048, p * 2048) + 64 * (np.arange(64) % 32)
    cur64 = (tq // 64) * 64
    ok = sstart[None, :] <= tq[:, None]
    forced = (sstart[None, :] == 0) | (sstart[None, :] == cur64[:, None]) | (sstart[None, :] == cur64[:, None] - 64)
    forced &= ok
    c["M1"] = (ok & ~forced).astype(np.float32)
    c["Ctab"] = np.where(ok, np.where(forced, 1e4, 0.0), -1e30).astype(np.float32)
    n = np.arange(256)
    reg_start = np.where(n < 128, (1 - p) * 2048, p * 2048)
    cend = reg_start + 16 * (n % 128) + 31
    cend[127] = 1 << 30 if p == 0 else cend[127]
    cend[255] = 1 << 30
    cb = np.where(cend[:, None] <= tq[None, :], 0.0, NEG)
    c["CB"] = _bf(cb.reshape(2, 128, HALF))
    E = np.zeros((128, S), np.float32)
    E[np.arange(S) // 64, np.arange(S)] = 1.0
    c["Eall"] = _bf(E)
    m = np.arange(128)[:, None]
    q = np.arange(128)[None, :]
    WBt = np.zeros((128, 4, 128), np.float32)
    other_ok = (p == 1)
    WBt[:, 0, :] = np.where((m > q) & other_ok, 0.0, NEG)
    WBt[:, 1, :] = 0.0 if other_ok else NEG
    WBt[:, 2, :] = np.where(m > q, 0.0, NEG)
    WBt[:, 3, :] = np.where(m <= q, 0.0, NEG)
    c["WB"] = _bf(WBt)
    cs_, ce_ = 16 * n, 16 * n + 31
    ss_ = 64 * np.arange(64)
    ov = np.minimum(ce_[:, None], ss_[None, :] + 63) - np.maximum(cs_[:, None], ss_[None, :]) + 1
    c["ovl"] = _bf((np.maximum(ov, 0) / 32.0).reshape(2, 128, 64))
    return c


def host_inputs(inputs):
    x = np.asarray(inputs["x"], np.float32)
    mem = np.asarray(inputs["mem"], np.float32)
    pos = np.asarray(inputs["positions"], np.int32)
    shared = {}
    shared["w_in0"] = np.ascontiguousarray(inputs["a_w_in"][0], np.float32)
    bw = np.asarray(inputs["b_w_in"][0], np.float32)
    shared["w_in1"] = np.ascontiguousarray(np.concatenate([bw[:, 0:1536], bw[:, 1572:3108], bw[:, 3108:3620], bw[:, 3620:4132]], axis=1))
    shared["w_gl"] = np.ascontiguousarray(bw[:, 1536:1572])
    shared["w_out0"] = np.ascontiguousarray(inputs["w_out"][0], np.float32)
    shared["w_out1"] = np.ascontiguousarray(inputs["w_out"][1], np.float32)
    shared["w_kv"] = np.ascontiguousarray(inputs["w_kv"], np.float32)
    shared["w_mkv0"] = np.ascontiguousarray(inputs["w_mem_kv"][0], np.float32)
    shared["w_mkv1"] = np.ascontiguousarray(inputs["w_mem_kv"][1], np.float32)
    shared["w_pool"] = np.ascontiguousarray(np.asarray(inputs["a_w_pool"][0], np.float32).reshape(1536, 384))
    shared["cw1"] = np.ascontiguousarray(np.asarray(inputs["cmp_w1"], np.float32).reshape(8192, 256))
    shared["cw2"] = np.ascontiguousarray(np.asarray(inputs["cmp_w2"], np.float32).reshape(512, 128))
    fm = lambda v: np.asarray(v, np.float32).reshape(16, 128).T
    ng, mg = np.asarray(inputs["norm_g"]), np.asarray(inputs["mem_norm_g"])
    shared["gT"] = np.ascontiguousarray(np.stack([fm(ng[0]), fm(ng[1]), fm(inputs["kv_norm_g"]), fm(mg[0]), fm(mg[1])], axis=1))
    shared["gfin"] = np.asarray(inputs["final_g"], np.float32).reshape(1, D)
    shared["pscale"] = np.ascontiguousarray(np.asarray(inputs["a_pool_scale"][0], np.float32).reshape(12, 128).T)
    shared["peT"] = np.ascontiguousarray(np.asarray(inputs["cmp_pe"], np.float32).transpose(2, 0, 1))
    consts = [host_consts(0), host_consts(1)]
    maps = []
    for b in range(4):
        for p in range(2):
            own = slice(2048 * p, 2048 * p + 2048)
            oth = slice(2048 * (1 - p), 2048 * (1 - p) + 2048)
            m = dict(shared)
            m.update(consts[p])
            m["x"] = np.ascontiguousarray(np.concatenate([x[b, oth], x[b, own]], axis=0))
            m["pos"] = np.ascontiguousarray(np.concatenate([pos[b, oth], pos[b, own]])[None, :])
            m["mem"] = np.ascontiguousarray(mem[b])
            maps.append(m)
    return maps


_NC_CACHE = {}


def kernel(**inputs):
    dbg = tuple(x for x in os.environ.get("K_DBG", "").split(",") if x)
    maps = host_inputs(inputs)
    if dbg not in _NC_CACHE:
        _NC_CACHE[dbg] = build(dbg)
    nc = _NC_CACHE[dbg]
    ncores = int(os.environ.get("K_NCORES", "8"))
    res = run_bass_kernel_spmd(nc, maps[:ncores], core_ids=list(range(ncores)))
    if dbg:
        return res
    out = np.zeros((4, S, D), np.float32)
    for b in range(4):
        for p in range(2):
            out[b, 2048 * p:2048 * p + 2048] = res.results[2 * b + p]["out"]
    return out
```
